# Optimizing a Trainium2 kernel written in Bass

```python
import math
import jax, jax.numpy as jnp
from jax import lax
import numpy as np

D_MODEL = 1024
BATCH = 4
SEQ = 4096
DEPTH = 1
DEC_BATCH = 128
DEC_SEQ = 1
PAST_LEN = 8192
PAGE_SIZE = 128

D_MIX = D_MODEL
M_WIDTH = D_MIX // 2
M_HEADS = 4
M_HD = M_WIDTH // M_HEADS
M_CHUNK = 128
A_WIDTH = D_MIX - M_WIDTH
A_HEADS = 8
A_HD = A_WIDTH // A_HEADS
A_KV = 2
A_GROUP = A_HEADS // A_KV
A_KVW = A_KV * A_HD
CMP_LEN = 32
CMP_STRIDE = 16
SLC_BLOCK = 64
N_SEL = 16
SLC_Q_BLK = 64
WINDOW = 512
Q_BLK = 128
N_BUCKETS = 32
MAX_DIST = 128
LN_EPS = 1e-5
ATT_SCALE = A_HD ** -0.5
DEEPNORM_ALPHA = (2.0 * DEPTH) ** 0.25
DEEPNORM_BETA = (8.0 * DEPTH) ** -0.25
IN_SPLITS = (M_WIDTH,) * 5 + (M_HEADS, M_HEADS) + (A_WIDTH,) + (A_KVW,) * 6 + (3 * A_HEADS, A_WIDTH)
D_IN = sum(IN_SPLITS)
F_GATE_OFF = 5 * M_WIDTH + M_HEADS

kernel_name = 'hymba_mlstm_nsa_deepnorm_step'


def layer_norm(x, g=None, b=None):
    xf = x.astype(jnp.float32)
    mu = xf.mean(-1, keepdims=True)
    var = jnp.square(xf - mu).mean(-1, keepdims=True)
    y = (xf - mu) * lax.rsqrt(var + LN_EPS)
    if g is not None:
        y = y * g + b
    return y


def masked_softmax(s, mask):
    s = jnp.where(mask, s.astype(jnp.float32), -jnp.inf)
    m = jnp.max(s, axis=-1, keepdims=True)
    m = jnp.where(jnp.isfinite(m), m, 0.0)
    e = jnp.exp(s - m)
    tot = e.sum(-1, keepdims=True)
    return e / jnp.where(tot > 0, tot, 1.0)


def t5_bucket(dist):
    n = jnp.maximum(dist, 0)
    max_exact = N_BUCKETS // 2
    nf = jnp.maximum(n, 1).astype(jnp.float32)
    large = max_exact + (jnp.log(nf / max_exact) / math.log(MAX_DIST / max_exact)
                         * (N_BUCKETS - max_exact)).astype(jnp.int32)
    large = jnp.minimum(large, N_BUCKETS - 1)
    return jnp.where(n < max_exact, n, large)


def adaln_project(x, c, w_ada, b_ada, w_in, b_in):
    mod = jax.nn.silu(c.astype(jnp.float32)) @ w_ada + b_ada
    shift, scale, gate = jnp.split(mod[:, None, :], 3, axis=-1)
    h = layer_norm(x) * (1.0 + scale) + shift
    split_points = np.cumsum(IN_SPLITS)[:-1].tolist()
    parts = jnp.split(h @ w_in + b_in, split_points, axis=-1)
    return parts, gate


def residual_out(x, mix, gate, w_out, b_out, ln_g, ln_b):
    y = mix @ w_out + b_out
    return layer_norm(DEEPNORM_ALPHA * x.astype(jnp.float32) + gate * y, ln_g, ln_b)


def mlstm_chunk(carry, inp):
    C, n, m = carry
    q, k, v, ig, lf = inp
    L = q.shape[2]
    b = jnp.cumsum(lf, axis=-1)
    inter = b + m[..., None]
    D = b[..., :, None] - b[..., None, :] + ig[..., None, :]
    D = jnp.where(jnp.tril(jnp.ones((L, L), dtype=bool)), D, -jnp.inf)
    m_t = jnp.maximum(inter, D.max(-1))
    w_inter = jnp.exp(inter - m_t)
    qk = jnp.einsum('bhtd,bhsd->bhts', q, k) * jnp.exp(D - m_t[..., None])
    num = w_inter[..., None] * jnp.einsum('bhtd,bhde->bhte', q, C) + jnp.einsum('bhts,bhse->bhte', qk, v)
    den = w_inter * jnp.einsum('bhtd,bhd->bht', q, n) + qk.sum(-1)
    h = num / jnp.maximum(jnp.abs(den), jnp.exp(-m_t))[..., None]
    m_new = m_t[..., -1]
    w_c = jnp.exp(inter[..., -1] - m_new)
    w_s = jnp.exp(b[..., -1:] - b + ig - m_new[..., None])
    C_new = w_c[..., None, None] * C + jnp.einsum('bhs,bhsd,bhse->bhde', w_s, k, v)
    n_new = w_c[..., None] * n + jnp.einsum('bhs,bhsd->bhd', w_s, k)
    return (C_new, n_new, m_new), h


def mlstm_mixer(q, k, v, o_pre, i_pre, f_pre, norm_g, carry, chunk):
    B, T, _ = q.shape
    nc = T // chunk

    def heads(a):
        return a.astype(jnp.float32).reshape(B, nc, chunk, M_HEADS, M_HD).transpose(1, 0, 3, 2, 4)

    def gates(a):
        return a.astype(jnp.float32).reshape(B, nc, chunk, M_HEADS).transpose(1, 0, 3, 2)

    inp = (heads(q), heads(k) * (M_HD ** -0.5), heads(v), gates(i_pre), jax.nn.log_sigmoid(gates(f_pre)))
    carry, h = lax.scan(mlstm_chunk, carry, inp)
    h = h.transpose(1, 0, 3, 2, 4).reshape(B, T, M_HEADS, M_HD)
    h = layer_norm(h) * norm_g.reshape(M_HEADS, M_HD)
    h = h.reshape(B, T, M_WIDTH)
    return jax.nn.sigmoid(o_pre.astype(jnp.float32)) * h, carry


def kv_rows(k, v):
    B, T, _ = k.shape
    return jnp.stack([k.reshape(B, T, A_KV, A_HD), v.reshape(B, T, A_KV, A_HD)], axis=3)


def compress(rows, pe, w1, b1, w2):
    B, L = rows.shape[:2]
    n_half = L // CMP_STRIDE
    r = CMP_LEN // CMP_STRIDE
    n_cmp = n_half - r + 1
    halves = rows[:, :n_half * CMP_STRIDE].astype(jnp.float32).reshape(B, n_half, CMP_STRIDE, A_KV, 2, A_HD)
    pre = b1
    for j in range(r):
        wj = w1[:, j * CMP_STRIDE:(j + 1) * CMP_STRIDE]
        pj = pe[:, j * CMP_STRIDE:(j + 1) * CMP_STRIDE]
        proj = jnp.einsum('bnpkcd,cpdh->bnkch', halves, wj) + jnp.einsum('cpd,cpdh->ch', pj, wj)
        pre = pre + proj[:, j:j + n_cmp]
    return jnp.einsum('bnkch,che->bnkce', jax.nn.gelu(pre), w2)


def cmp_attend(q, kc, t_pos, rel_bias):
    T, N = q.shape[1], kc.shape[1]
    ends = jnp.arange(N, dtype=jnp.int32) * CMP_STRIDE + (CMP_LEN - 1)
    dist = t_pos[:, None] - ends[None, :]
    bias = rel_bias[t5_bucket(dist)].reshape(T, N, A_KV, A_GROUP).transpose(2, 3, 0, 1)
    s = jnp.einsum('btkgd,bnkd->bkgtn', q, kc[..., 0, :]) * ATT_SCALE + bias
    p = masked_softmax(s, dist >= 0)
    o = jnp.einsum('bkgtn,bnkd->btkgd', p, kc[..., 1, :])
    return o, p.sum(axis=2)


def select_blocks(imp, t_pos, L):
    n_cmp = imp.shape[-1]
    n_slc = -(-L // SLC_BLOCK)
    c_start = jnp.arange(n_cmp) * CMP_STRIDE
    c_end = c_start + CMP_LEN - 1
    s_start = jnp.arange(n_slc) * SLC_BLOCK
    s_end = s_start + SLC_BLOCK - 1
    cover = ((c_start[:, None] <= s_end[None]) & (c_end[:, None] >= s_start[None])).astype(jnp.float32)
    score = imp @ cover
    blk = jnp.arange(n_slc)[None, :]
    cur = (t_pos // SLC_BLOCK)[:, None]
    valid = s_start[None, :] <= t_pos[:, None]
    forced = (blk == 0) | (blk == cur) | (blk == cur - 1)
    score = jnp.where(forced, jnp.inf, jnp.where(valid, score, -jnp.inf))
    _, idx = lax.top_k(score, min(N_SEL, n_slc))
    return idx


def slc_attend(q, rows, pos, t_pos, rel_bias):
    dist = t_pos[None, None, :, None] - pos
    rb = rel_bias.reshape(N_BUCKETS, A_KV, A_GROUP)
    kv_i = jnp.arange(A_KV)[None, :, None, None]
    bias = jnp.moveaxis(rb[t5_bucket(dist), kv_i], -1, 2)
    rows = rows.astype(jnp.float32)
    s = jnp.einsum('btkgd,bktmd->bkgtm', q, rows[..., 0, :]) * ATT_SCALE + bias
    p = masked_softmax(s, (dist >= 0)[:, :, None])
    return jnp.einsum('bkgtm,bktmd->btkgd', p, rows[..., 1, :])


def window_prompt(q, win_rows, rel_bias):
    B, T = q.shape[:2]
    nb = T // Q_BLK
    span = WINDOW + Q_BLK
    pad = jnp.pad(win_rows.astype(jnp.float32), ((0, 0), (WINDOW, 0), (0, 0), (0, 0), (0, 0)))
    rel = jnp.arange(Q_BLK)[:, None] - jnp.arange(span)[None, :] + WINDOW
    band_mask = (rel >= 0) & (rel <= WINDOW)
    bias = rel_bias[t5_bucket(rel)].reshape(Q_BLK, span, A_KV, A_GROUP).transpose(2, 3, 0, 1)
    q_blk = jnp.moveaxis(q.reshape(B, nb, Q_BLK, A_KV, A_GROUP, A_HD), 1, 0)

    def one(args):
        qb, b = args
        band = lax.dynamic_slice_in_dim(pad, b * Q_BLK, span, axis=1)
        s_pos = b * Q_BLK - WINDOW + jnp.arange(span)
        mask = band_mask & (s_pos >= 0)[None, :]
        s = jnp.einsum('bqkgd,bmkd->bkgqm', qb, band[..., 0, :]) * ATT_SCALE + bias
        p = masked_softmax(s, mask)
        return jnp.einsum('bkgqm,bmkd->bqkgd', p, band[..., 1, :])

    o = lax.map(one, (q_blk, jnp.arange(nb)))
    return jnp.moveaxis(o, 0, 1).reshape(B, T, A_KV, A_GROUP, A_HD)


def window_sample(q, keys, t_pos, s0, rel_bias):
    T, M = q.shape[1], keys.shape[1]
    keys = keys.astype(jnp.float32)
    rel = t_pos[:, None] - (s0 + jnp.arange(M))[None, :]
    mask = (rel >= 0) & (rel <= WINDOW)
    bias = rel_bias[t5_bucket(rel)].reshape(T, M, A_KV, A_GROUP).transpose(2, 3, 0, 1)
    s = jnp.einsum('btkgd,bmkd->bkgtm', q, keys[..., 0, :]) * ATT_SCALE + bias
    p = masked_softmax(s, mask)
    return jnp.einsum('bkgtm,bmkd->btkgd', p, keys[..., 1, :])


def gate_merge(ga, o_c, o_s, o_w):
    B, T, _ = ga.shape
    g = jax.nn.sigmoid(ga.astype(jnp.float32)).reshape(B, T, 3, A_KV, A_GROUP, 1)
    o = g[:, :, 0] * o_c + g[:, :, 1] * o_s + g[:, :, 2] * o_w
    return o.reshape(B, T, A_WIDTH)


def nsa_prompt(qa, cmp_rows, slc_rows, win_rows, ga, cmp_params, rel_bias):
    B, T, _ = qa.shape
    q = qa.astype(jnp.float32).reshape(B, T, A_KV, A_GROUP, A_HD)
    t_pos = jnp.arange(T, dtype=jnp.int32)
    kc = compress(cmp_rows, *cmp_params)
    o_c, imp = cmp_attend(q, kc, t_pos, rel_bias)
    idx = select_blocks(imp, t_pos, T)
    n_slc = T // SLC_BLOCK
    blocks = slc_rows.astype(jnp.float32).transpose(0, 2, 1, 3, 4).reshape(B, A_KV, n_slc, SLC_BLOCK, 2, A_HD)
    b_i = jnp.arange(B)[:, None, None, None]
    kv_i = jnp.arange(A_KV)[None, :, None, None]
    nqb = T // SLC_Q_BLK

    def one(args):
        qb, ib, tb = args
        kk = ib.shape[-1]
        rows = blocks[b_i, kv_i, ib].reshape(B, A_KV, SLC_Q_BLK, kk * SLC_BLOCK, 2, A_HD)
        pos = (ib[..., None] * SLC_BLOCK + jnp.arange(SLC_BLOCK)).reshape(B, A_KV, SLC_Q_BLK, kk * SLC_BLOCK)
        return slc_attend(qb, rows, pos, tb, rel_bias)

    q_blk = jnp.moveaxis(q.reshape(B, nqb, SLC_Q_BLK, A_KV, A_GROUP, A_HD), 1, 0)
    i_blk = jnp.moveaxis(idx.reshape(B, A_KV, nqb, SLC_Q_BLK, idx.shape[-1]), 2, 0)
    t_blk = t_pos.reshape(nqb, SLC_Q_BLK)
    o_s = lax.map(one, (q_blk, i_blk, t_blk))
    o_s = jnp.moveaxis(o_s, 0, 1).reshape(B, T, A_KV, A_GROUP, A_HD)
    o_w = window_prompt(q, win_rows, rel_bias)
    return gate_merge(ga, o_c, o_s, o_w)


def nsa_sample(qa, cmp_new, slc_new, win_new, ga, cache_cmp_kv, cache_slc_kv, win_cache,
               page_table, layer, cmp_params, rel_bias):
    DB, T, _ = qa.shape
    P = PAST_LEN
    L = P + T
    q = qa.astype(jnp.float32).reshape(DB, T, A_KV, A_GROUP, A_HD)
    t_pos = P + jnp.arange(T, dtype=jnp.int32)
    past_cmp = cache_cmp_kv[layer, page_table].reshape(DB, P, A_KV, 2, A_HD)
    kc = compress(jnp.concatenate([past_cmp, cmp_new], axis=1), *cmp_params)
    o_c, imp = cmp_attend(q, kc, t_pos, rel_bias)
    idx = select_blocks(imp, t_pos, L)
    kk = idx.shape[-1]
    pos = (idx[..., None] * SLC_BLOCK + jnp.arange(SLC_BLOCK)).reshape(DB, A_KV, T, kk * SLC_BLOCK)
    b_i = jnp.arange(DB)[:, None, None, None]
    kv_i = jnp.arange(A_KV)[None, :, None, None]
    past_pos = jnp.minimum(pos, P - 1)
    phys = page_table[b_i, past_pos // PAGE_SIZE]
    rows_past = cache_slc_kv[layer, phys, past_pos % PAGE_SIZE, kv_i]
    rows_new = slc_new[b_i, jnp.clip(pos - P, 0, T - 1), kv_i]
    rows = jnp.where((pos < P)[..., None, None], rows_past, rows_new)
    o_s = slc_attend(q, rows, pos, t_pos, rel_bias)
    keys = jnp.concatenate([win_cache, win_new], axis=1)
    o_w = window_sample(q, keys, t_pos, P - win_cache.shape[1], rel_bias)
    return gate_merge(ga, o_c, o_s, o_w), keys[:, T:]


def setup_inputs(seed: int = 0) -> dict:
    key = jax.random.key(seed)
    ks = jax.random.split(key, 26)
    f32 = jnp.float32
    n_pages = PAST_LEN // PAGE_SIZE
    n_phys = (DEC_BATCH * n_pages * 5) // 4
    win_buf = min(WINDOW, PAST_LEN)

    def nrm(k, shape, s):
        return s * jax.random.normal(k, shape, f32)

    b_in = nrm(ks[13], (DEPTH, D_IN), 0.01)
    b_in = b_in.at[:, F_GATE_OFF:F_GATE_OFF + M_HEADS].add(jnp.linspace(3.0, 6.0, M_HEADS))
    page_table = jax.random.permutation(ks[8], n_phys)[:DEC_BATCH * n_pages].reshape(DEC_BATCH, n_pages).astype(jnp.int32)
    return {
        'x_prompt': nrm(ks[0], (BATCH, SEQ, D_MODEL), 1.0),
        'x_sample': nrm(ks[1], (DEC_BATCH, DEC_SEQ, D_MODEL), 1.0),
        'cache_cmp_kv': nrm(ks[2], (DEPTH, n_phys, PAGE_SIZE, A_KV, 2, A_HD), 1.0),
        'cache_slc_kv': nrm(ks[3], (DEPTH, n_phys, PAGE_SIZE, A_KV, 2, A_HD), 1.0),
        'cache_win_kv': nrm(ks[4], (DEPTH, DEC_BATCH, win_buf, A_KV, 2, A_HD), 1.0),
        'state_mlstm_C': nrm(ks[5], (DEPTH, DEC_BATCH, M_HEADS, M_HD, M_HD), 0.5),
        'state_mlstm_n': nrm(ks[6], (DEPTH, DEC_BATCH, M_HEADS, M_HD), 0.5),
        'state_mlstm_m': nrm(ks[7], (DEPTH, DEC_BATCH, M_HEADS), 1.0),
        'page_table': page_table,
        'c_prompt': nrm(ks[9], (BATCH, D_MODEL), 1.0),
        'c_sample': nrm(ks[10], (DEC_BATCH, D_MODEL), 1.0),
        'rel_bias': nrm(ks[11], (N_BUCKETS, A_HEADS), 0.5),
        'w_ada': nrm(ks[12], (DEPTH, D_MODEL, 3 * D_MODEL), 0.5 * D_MODEL ** -0.5),
        'b_ada': nrm(ks[14], (DEPTH, 3 * D_MODEL), 0.01),
        'w_in': nrm(ks[15], (DEPTH, D_MODEL, D_IN), D_MODEL ** -0.5),
        'b_in': b_in,
        'm_norm_g': 1.0 + nrm(ks[16], (DEPTH, M_WIDTH), 0.01),
        'cmp_pe': nrm(ks[17], (DEPTH, 2, CMP_LEN, A_HD), 0.02),
        'cmp_w1': nrm(ks[18], (DEPTH, 2, CMP_LEN, A_HD, A_HD), (CMP_LEN * A_HD) ** -0.5),
        'cmp_b1': nrm(ks[19], (DEPTH, 2, A_HD), 0.01),
        'cmp_w2': nrm(ks[20], (DEPTH, 2, A_HD, A_HD), A_HD ** -0.5),
        'w_out': nrm(ks[21], (DEPTH, D_MIX, D_MODEL), DEEPNORM_BETA * D_MIX ** -0.5),
        'b_out': nrm(ks[22], (DEPTH, D_MODEL), 0.01),
        'ln_g': 1.0 + nrm(ks[23], (DEPTH, D_MODEL), 0.01),
        'ln_b': nrm(ks[24], (DEPTH, D_MODEL), 0.01),
    }


def reference(x_prompt, x_sample, cache_cmp_kv, cache_slc_kv, cache_win_kv,
              state_mlstm_C, state_mlstm_n, state_mlstm_m, page_table,
              c_prompt, c_sample, rel_bias, w_ada, b_ada, w_in, b_in, m_norm_g,
              cmp_pe, cmp_w1, cmp_b1, cmp_w2, w_out, b_out, ln_g, ln_b):
    f32 = jnp.float32
    win_buf = min(WINDOW, PAST_LEN)
    x_p, x_s = x_prompt, x_sample
    outs = [[] for _ in range(12)]
    for l in range(DEPTH):
        cmp_params = (cmp_pe[l], cmp_w1[l], cmp_b1[l], cmp_w2[l])
        parts, gate = adaln_project(x_p, c_prompt, w_ada[l], b_ada[l], w_in[l], b_in[l])
        qm, km, vm, om, im, fm, zm = parts[0], parts[1], parts[2], parts[3], parts[5], parts[6], parts[4]
        qa, ck, cv, sk, sv, wk, wv, ga, za = parts[7:]
        B, T = x_p.shape[:2]
        carry0 = (jnp.zeros((B, M_HEADS, M_HD, M_HD), f32), jnp.zeros((B, M_HEADS, M_HD), f32),
                  jnp.zeros((B, M_HEADS), f32))
        hm, (C_p, n_p, m_p) = mlstm_mixer(qm, km, vm, om, im, fm, m_norm_g[l], carry0, M_CHUNK)
        cmp_p, slc_p, win_p = kv_rows(ck, cv), kv_rows(sk, sv), kv_rows(wk, wv)
        ha = nsa_prompt(qa, cmp_p, slc_p, win_p, ga, cmp_params, rel_bias)
        mix = jnp.concatenate([hm * jax.nn.silu(zm), ha * jax.nn.silu(za)], axis=-1)
        x_p = residual_out(x_p, mix, gate, w_out[l], b_out[l], ln_g[l], ln_b[l])
        wbuf_p = jnp.pad(win_p, ((0, 0), (max(win_buf - T, 0), 0), (0, 0), (0, 0), (0, 0)))[:, -win_buf:]
        parts, gate = adaln_project(x_s, c_sample, w_ada[l], b_ada[l], w_in[l], b_in[l])
        qm, km, vm, om, im, fm, zm = parts[0], parts[1], parts[2], parts[3], parts[5], parts[6], parts[4]
        qa, ck, cv, sk, sv, wk, wv, ga, za = parts[7:]
        carry = (state_mlstm_C[l].astype(f32), state_mlstm_n[l].astype(f32), state_mlstm_m[l].astype(f32))
        hm, (C_s, n_s, m_s) = mlstm_mixer(qm, km, vm, om, im, fm, m_norm_g[l], carry, x_s.shape[1])
        cmp_s, slc_s, win_s = kv_rows(ck, cv), kv_rows(sk, sv), kv_rows(wk, wv)
        ha, wbuf_s = nsa_sample(qa, cmp_s, slc_s, win_s, ga, cache_cmp_kv, cache_slc_kv, cache_win_kv[l],
                                page_table, l, cmp_params, rel_bias)
        mix = jnp.concatenate([hm * jax.nn.silu(zm), ha * jax.nn.silu(za)], axis=-1)
        x_s = residual_out(x_s, mix, gate, w_out[l], b_out[l], ln_g[l], ln_b[l])
        for o, a in zip(outs, (cmp_p, cmp_s, slc_p, slc_s, wbuf_p, wbuf_s, C_p, C_s, n_p, n_s, m_p, m_s)):
            o.append(a)
    new_state = [jnp.stack(o) for o in outs]
    return (x_p, x_s, *new_state)
```

```python
from contextlib import ExitStack
import numpy as np
import concourse.bass as bass
import concourse.mybir as mybir
from concourse.bass_utils import run_bass_kernel_spmd

F32 = mybir.dt.float32
BF16 = mybir.dt.bfloat16
I32 = mybir.dt.int32
U32 = mybir.dt.uint32
AF = mybir.ActivationFunctionType
ALU = mybir.AluOpType
AX = mybir.AxisListType

NCORES = 8
D = 1024
T = 4096
NT = T // 128
DIN = 4384
SB = 16
LN_EPS = 1e-5
NCONST = 512 + 128 + 128
import os
NT_RUN = int(os.environ.get('DBG_NT', NT))
DBG_ST = int(os.environ.get('DBG_ST', 99))
DBG_X = int(os.environ.get('DBG_X', 0))
NEG = -30000.0
NPHYS = 10240
NCS = 32 + 1 + 516 + 1024 + 128
BIG = 1.0e30
ATT_SCALE = 0.125
ALPHA = 2.0 ** 0.25
O_QM, O_KM, O_VM, O_OM, O_ZM = 0, 512, 1024, 1536, 2048
O_IM, O_FM = 2560, 2564
O_QA = 2568
O_CK, O_CV, O_SK, O_SV, O_WK, O_WV = 3080, 3208, 3336, 3464, 3592, 3720
O_GA = 3848
O_ZA = 3872


class _Q:
    def __init__(self, eng):
        self.eng = eng
        self.n = 0

    def append(self, f):
        self.n += 1
        f(self.eng)

    def __len__(self):
        return self.n


class Sched:
    def __init__(self, nc, es):
        self.nc = nc
        self.es = es
        self.q = {'pe': _Q(nc.tensor), 'act': _Q(nc.scalar), 'dve': _Q(nc.vector), 'pool': _Q(nc.gpsimd), 'sp': _Q(nc.sync)}
        self.semh = {}
        self.semcnt = {}
        self.waited = {e: {} for e in self.q}
        self.lastw = {}
        self.readers = {}
        for e in self.q:
            self._mksem(e)

    def _mksem(self, key):
        if key not in self.semh:
            self.semh[key] = self.es.enter_context(self.nc.semaphore('s_' + key))
            self.semcnt[key] = 0
        return self.semh[key]

    def _deps(self, e, reads, writes):
        need = {}

        def add(d):
            for k, v in d.items():
                if need.get(k, 0) < v:
                    need[k] = v
        for r in reads:
            add(self.lastw.get(r, {}))
            if r in self.excl:
                add({k: v for k, v in self.readers.get(r, {}).items() if k != e})
        for w in writes:
            add(self.lastw.get(w, {}))
            add(self.readers.get(w, {}))
        for k, v in need.items():
            if k == 'pe' and e == 'pe':
                continue
            if self.waited[e].get(k, 0) < v:
                self.waited[e][k] = v
                h = self.semh[k]
                self.q[e].append(lambda eng, h=h, v=v: eng.wait_ge(h, v))

    def _commit(self, ev, reads, writes, acc):
        k, v = ev
        for r in reads:
            d = self.readers.setdefault(r, {})
            d[k] = max(d.get(k, 0), v)
        for w in writes:
            if acc:
                d = self.lastw.setdefault(w, {})
                d[k] = max(d.get(k, 0), v)
            else:
                self.lastw[w] = {k: v}
                self.readers[w] = {}

    skip = False
    excl = frozenset(['pT', 'pacc0', 'pacc1', 'pm', 'pS', 'pN', 'pC', 'pD'])

    def op(self, e, fn, reads=(), writes=(), acc=False):
        if self.skip:
            return
        self._deps(e, reads, writes)
        self.semcnt[e] += 1
        v = self.semcnt[e]
        h = self.semh[e]
        self.q[e].append(lambda eng, fn=fn, h=h: fn(eng).then_inc(h, 1))
        self._commit((e, v), reads, writes, acc)

    def dma(self, e, fn, semkey, reads=(), writes=(), acc=True):
        if self.skip:
            return
        self._deps(e, reads, writes)
        k = 'd_' + semkey
        self._mksem(k)
        self.semcnt[k] += 16
        v = self.semcnt[k]
        h = self.semh[k]
        self.q[e].append(lambda eng, fn=fn, h=h: fn(eng).then_inc(h, 16))
        self._commit((k, v), reads, writes, acc)

    def barrier(self):
        for e in self.q:
            for k, v in self.semcnt.items():
                if v > 0 and self.waited[e].get(k, 0) < v and not (k == e == 'pe'):
                    self.waited[e][k] = v
                    h = self.semh[k]
                    self.q[e].append(lambda eng, h=h, v=v: eng.wait_ge(h, v))

    def finish(self):
        for k, v in self.semcnt.items():
            if k.startswith('d_') and v > 0:
                h = self.semh[k]
                self.q['sp'].append(lambda eng, h=h, v=v: eng.wait_ge(h, v))

    def emit(self):
        pass


def build_nc():
    nc = bass.Bass("TRN2", target_bir_lowering=False)
    es = ExitStack()

    def din(name, shape, dt=F32):
        return nc.dram_tensor(name, list(shape), dt, kind="ExternalInput").ap()

    def dout(name, shape, dt=F32):
        return nc.dram_tensor(name, list(shape), dt, kind="ExternalOutput").ap()

    x_p = din("x_p", [T, D])
    x_s = din("x_s", [SB, D])
    c_p = din("c_p", [1, D])
    c_s = din("c_s", [SB, D])
    w_ada = din("w_ada", [D, 3 * D])
    b_ada = din("b_ada", [1, 3 * D])
    w_in = din("w_in", [D, DIN])
    b_in = din("b_in", [1, DIN])
    consts_d = din("consts_in", [128, NCONST])
    win_c = din("win_c", [SB, 512, 256])

    o_cmp_p = dout("o_cmp_p", [T, 256])
    o_slc_p = dout("o_slc_p", [T, 256])
    o_win_p = dout("o_win_p", [512, 256])
    o_cmp_s = dout("o_cmp_s", [SB, 256])
    o_slc_s = dout("o_slc_s", [SB, 256])
    o_win_s = dout("o_win_s", [SB, 512, 256])
    o_C_p = dout("o_C_p", [4, 128, 128])
    o_n_p = dout("o_n_p", [4, 128])
    o_m_p = dout("o_m_p", [1, 4])
    m_norm_g = din("m_norm_g", [1, 512])
    st_C = din("st_C", [SB, 4, 128, 128])
    rel_bias = din("rel_bias", [1, 256])
    eblk_d = din("eblk", [128, T])
    oh_dt = din("oh_dt", [2, 128, 32 * 128])
    am_dt = din("am_dt", [128, 3 * 128])
    oh_c = din("oh_c", [128, 32 * 16])
    am_c = din("am_c", [128, 16])
    cmp_w1 = din("cmp_w1", [2, 32, 64, 64])
    cmp_pe = din("cmp_pe", [2, 32, 64])
    cmp_b1 = din("cmp_b1", [2, 64])
    cmp_w2 = din("cmp_w2", [2, 64, 64])
    w_out = din("w_out", [D, D])
    b_out = din("b_out", [1, D])
    ln_g = din("ln_g", [1, D])
    ln_b = din("ln_b", [1, D])
    o_y_p = dout("o_y_p", [T, D])
    o_y_s = dout("o_y_s", [SB, D])
    cache_cmp = din("cache_cmp", [NPHYS * 4, 8192])
    cache_slc = din("cache_slc", [NPHYS * 4, 8192])
    page_tab = din("page_tab", [1, SB * 64], I32)
    consts_s_d = din("consts_s", [128, NCS])
    ohcs_d = din("ohcs_in", [32, 512])
    o_scr = nc.dram_tensor("o_scr", [3, SB, 2, 4, 65], F32, kind="Internal").ap()
    st_n = din("st_n", [SB, 512])
    st_m = din("st_m", [SB, 4])
    o_C_s = dout("o_C_s", [SB, 4, 128, 128])
    o_n_s = dout("o_n_s", [SB, 512])
    o_m_s = dout("o_m_s", [SB, 4])

    S = Sched(nc, es)

    def sb(name, shape, dt=F32, st=None):
        return (st or es).enter_context(nc.sbuf_tensor(name, list(shape), dt))

    def ps(name, shape, dt=F32):
        return es.enter_context(nc.psum_tensor(name, list(shape), dt))

    NR = 33
    PR = 32
    CH = [(c0, min(512, DIN - c0)) for c0 in range(0, DIN, 512)]
    consts = sb("consts", [128, NCONST])
    ident_f = consts[:, 0:128]
    tri_f = consts[:, 128:256]
    i16bc = consts[:, 256:512].rearrange("p (a b) -> p a b", a=16)
    cover_f = consts[:, 512:640].rearrange("p (c b) -> p c b", c=2)
    fv_ext = consts[:, 640:768]
    ident = sb("ident", [128, 128], BF16)
    onesf = sb("onesf", [128, 128])
    w_in_d = nc.dram_tensor("w_in_bf_d", [9, 128, 8 * 512], BF16, kind="Internal").ap()
    wbuf = [sb("wbuf0", [128, 8, 512], BF16), sb("wbuf1", [128, 8, 512], BF16)]
    bias3 = sb("bias3", [65, 3, 512], BF16)
    onesb = sb("onesb", [65, 128], BF16)
    lo8 = sb("lo8", [128, 8])
    gate_bc = sb("gate_bc", [128, D])
    modT = sb("modT", [128, 24])
    one1 = sb("one1", [NR, 128])
    pT = ps("pT", [128, 1024], BF16)
    pacc = [ps("pacc0", [128, 512]), ps("pacc1", [128, 512])]
    pm = ps("pm", [128, 512])
    w_out_bf = sb("w_out_bf", [128, 8, D], BF16)
    lng_bc = sb("lng_bc", [128, D])
    lnb_bc = sb("lnb_bc", [128, D])
    bout_bc = sb("bout_bc", [128, D])
    S.dma('sp', lambda e: e.dma_start(out=lng_bc[:], in_=ln_g.partition_broadcast(128)), 'lng_bc', writes=['lng_bc'])
    S.dma('sp', lambda e: e.dma_start(out=lnb_bc[:], in_=ln_b.partition_broadcast(128)), 'lnb_bc', writes=['lnb_bc'])
    S.dma('sp', lambda e: e.dma_start(out=bout_bc[:], in_=b_out.partition_broadcast(128)), 'bout_bc', writes=['bout_bc'])
    pS = ps("pS", [128, 512])
    pN = ps("pN", [128, 512])
    pC = ps("pC", [128, 512])
    pD = ps("pD", [128, 512])

    S.dma('sp', lambda e: e.dma_start(out=consts[:], in_=consts_d), 'ident_f', writes=['ident_f'])
    S.op('dve', lambda e: e.tensor_copy(out=ident[:], in_=ident_f), reads=['ident_f'], writes=['ident'])
    S.op('dve', lambda e: e.memset(onesf[:], 1.0), writes=['onesf'])
    S.op('dve', lambda e: e.memset(one1[:], 1.0), writes=['one1'])
    epsc = sb("epsc", [128, 1])
    ng_bc = sb("ng_bc", [128, 512])
    S.dma('sp', lambda e: e.dma_start(out=ng_bc[:], in_=m_norm_g.partition_broadcast(128)), 'ng_bc', writes=['ng_bc'])
    S.op('dve', lambda e: e.memset(epsc[:], LN_EPS), writes=['epsc'])
    S.op('dve', lambda e: e.memset(onesb[:], 1.0), writes=['onesb'])

    wcnt = [0]

    def in_proj(rows, lhs_fn, lhs_res, out_tile, out_res, evac_eng):
        for ci, (c0, w) in enumerate(CH):
            i = wcnt[0]
            wcnt[0] += 1
            wb = wbuf[i % 2]
            wn = 'wbuf%d' % (i % 2)
            S.dma('sp', lambda e, wb=wb, ci=ci: e.dma_start(out=wb[:].rearrange("p k c -> p (k c)"), in_=w_in_d[ci]), wn, writes=[wn], acc=False)
            pa = pacc[ci % 2]
            pn = 'pacc%d' % (ci % 2)
            for k in range(8):
                S.op('pe', lambda e, pa=pa, wb=wb, w=w, k=k: e.matmul(pa[0:rows, 0:w], lhsT=lhs_fn(k), rhs=wb[:, k, 0:w], start=(k == 0), stop=False),
                     reads=[lhs_res, wn], writes=[pn], acc=(k > 0))
            base = 32 * (ci % 3)
            S.op('pe', lambda e, pa=pa, w=w, base=base, ci=ci: e.matmul(pa[0:rows, 0:w], lhsT=onesb[base:base + 1, 0:rows], rhs=bias3[base:base + 1, ci // 3, 0:w], start=False, stop=True),
                 reads=['onesb', 'bias3'], writes=[pn], acc=True)
            if evac_eng == 'act':
                S.op('act', lambda e, pa=pa, c0=c0, w=w: e.copy(out=out_tile[0:rows, c0:c0 + w], in_=pa[0:rows, 0:w]), reads=[pn], writes=[out_res], acc=(ci > 0))
            else:
                S.op('dve', lambda e, pa=pa, c0=c0, w=w: e.tensor_copy(out=out_tile[0:rows, c0:c0 + w], in_=pa[0:rows, 0:w]), reads=[pn], writes=[out_res], acc=(ci > 0))
        S.op('dve', lambda e: e.tensor_tensor(out=out_tile[0:rows, O_IM:O_IM + 8], in0=out_tile[0:rows, O_IM:O_IM + 8], in1=lo8[0:rows, :], op=ALU.add),
             reads=[out_res, 'lo8'], writes=[out_res])

    def layer_norm_stats(xt, xname, rows, mv, st6, tag):
        S.op('dve', lambda e: e.bn_stats(out=st6[0:rows, 0, :], in_=xt[0:rows, 0:512]), reads=[xname], writes=['st6' + tag], acc=True)
        S.op('dve', lambda e: e.bn_stats(out=st6[0:rows, 1, :], in_=xt[0:rows, 512:1024]), reads=[xname], writes=['st6' + tag], acc=True)
        S.op('dve', lambda e: e.bn_aggr(out=mv[0:rows, 0:2], in_=st6[0:rows, :, :]), reads=['st6' + tag], writes=['mv' + tag])
        S.op('act', lambda e: e.activation(out=mv[0:rows, 2:3], in_=mv[0:rows, 1:2], func=AF.Sqrt, bias=epsc[0:rows, :], scale=1.0),
             reads=['mv' + tag, 'epsc'], writes=['mv' + tag])
        S.op('dve', lambda e: e.reciprocal(out=mv[0:rows, 2:3], in_=mv[0:rows, 2:3]), reads=['mv' + tag], writes=['mv' + tag])


    def load_cmp_weights(tmp, W1bf, cbias, w2p, sfx=''):
        w1st = sb("w1st" + sfx, [64, 2, 32, 64], st=tmp)
        S.dma('sp', lambda e: e.dma_start(out=w1st[:], in_=cmp_w1.rearrange("c p d h -> d c p h")), 'w1st', writes=['w1st'], acc=False)
        S.op('act', lambda e: e.copy(out=W1bf[:].rearrange("d c p h -> d (c p h)"), in_=w1st[:].rearrange("d c p h -> d (c p h)")), reads=['w1st'], writes=['W1bf'])
        pest = sb("pest" + sfx, [64, 2, 32], st=tmp)
        S.dma('sp', lambda e: e.dma_start(out=pest[:], in_=cmp_pe.rearrange("c p d -> d c p"), allow_slow_non_contiguous=True), 'pest', writes=['pest'], acc=False)
        b1st = sb("b1st" + sfx, [64, 2], st=tmp)
        S.dma('sp', lambda e: e.dma_start(out=b1st[:], in_=cmp_b1.rearrange("c h -> h c"), allow_slow_non_contiguous=True), 'b1st', writes=['b1st'], acc=False)
        for c in range(2):
            for p_ in range(32):
                S.op('pe', lambda e, c=c, p_=p_: e.matmul(pm[0:64, c:c + 1], lhsT=w1st[:, c, p_, :], rhs=pest[:, c, p_:p_ + 1], start=(p_ == 0), stop=(p_ == 31)),
                     reads=['w1st', 'pest'], writes=['pm'], acc=True)
        S.op('dve', lambda e: e.tensor_tensor(out=cbias[:], in0=pm[0:64, 0:2], in1=b1st[:], op=ALU.add), reads=['pm', 'b1st'], writes=['cbias'])
        w2st = sb("w2st" + sfx, [64, 2, 64], st=tmp)
        S.dma('sp', lambda e: e.dma_start(out=w2st[:], in_=cmp_w2.rearrange("c h e -> h c e")), 'w2st', writes=['w2st'], acc=False)
        S.op('dve', lambda e: e.memset(w2p[:], 0.0), writes=['w2p'])
        S.op('dve', lambda e: e.tensor_copy(out=w2p[:, :, 64:128], in_=w2st[:]), reads=['w2st', 'w2p'], writes=['w2p'], acc=True)

    mid = es.enter_context(ExitStack())
    mod_t = sb("mod_t", [NR, 3 * D], st=mid)
    proj_s = sb("proj_s", [SB, DIN], st=mid)

    with ExitStack() as ph:
        HW = DIN // 2
        stage = [sb("stage0", [128, HW], st=ph), sb("stage1", [128, HW], st=ph)]
        wcast = [sb("wcast0", [128, 512], BF16, st=ph), sb("wcast1", [128, 512], BF16, st=ph)]
        i = 0
        for ci, (c0, w) in enumerate(CH):
            for k in range(8):
                st = stage[i % 2]
                sn = 'stage%d' % (i % 2)
                wc = wcast[i % 2]
                wcn = 'wcast%d' % (i % 2)
                S.dma('sp', lambda e, st=st, k=k, c0=c0, w=w: e.dma_start(out=st[:, 0:w], in_=w_in[k * 128:(k + 1) * 128, c0:c0 + w]), sn, writes=[sn], acc=False)
                if w < 512:
                    S.op('dve', lambda e, wc=wc: e.memset(wc[:], 0.0), writes=[wcn])
                if i % 2 == 0:
                    S.op('act', lambda e, st=st, wc=wc, w=w: e.copy(out=wc[:, 0:w], in_=st[:, 0:w]), reads=[sn, wcn], writes=[wcn])
                else:
                    S.op('pool', lambda e, st=st, wc=wc, w=w: e.tensor_copy(out=wc[:, 0:w], in_=st[:, 0:w]), reads=[sn, wcn], writes=[wcn])
                S.dma('pool', lambda e, wc=wc, ci=ci, k=k: e.dma_start(out=w_in_d[ci].rearrange("p (k c) -> p k c", k=8)[:, k, :], in_=wc[:]), wcn + 'o', reads=[wcn], writes=['w_in_d'])
                i += 1
        b3f = sb("b3f", [65, 3, 512], st=ph)
        S.op('dve', lambda e: e.memset(b3f[:], 0.0), writes=['b3f'])
        for ci, (c0, w) in enumerate(CH):
            base = 32 * (ci % 3)
            S.dma('sp', lambda e, base=base, ci=ci, c0=c0, w=w: e.dma_start(out=b3f[base:base + 1, ci // 3, 0:w], in_=b_in[:, c0:c0 + w]), 'b3f', writes=['b3f'])
        S.op('dve', lambda e: e.tensor_copy(out=bias3[:], in_=b3f[:]), reads=['b3f'], writes=['bias3'])
        b8 = sb("b8", [128, 8], st=ph)
        b8b = sb("b8b", [128, 8], BF16, st=ph)
        S.dma('sp', lambda e: e.dma_start(out=b8[:], in_=b_in[:, O_IM:O_IM + 8].partition_broadcast(128)), 'b8', writes=['b8'])
        S.op('dve', lambda e: e.tensor_copy(out=b8b[:], in_=b8[:]), reads=['b8'], writes=['b8b'])
        S.op('dve', lambda e: e.tensor_tensor(out=lo8[:], in0=b8[:], in1=b8b[:], op=ALU.subtract), reads=['b8', 'b8b'], writes=['lo8'])
        for k in range(8):
            st = stage[k % 2]
            sn = 'stage%d' % (k % 2)
            S.dma('sp', lambda e, st=st, k=k: e.dma_start(out=st[:, 0:D], in_=w_out[k * 128:(k + 1) * 128, :]), sn, writes=[sn], acc=False)
            if k % 2 == 0:
                S.op('act', lambda e, st=st, k=k: e.copy(out=w_out_bf[:, k, :], in_=st[:, 0:D]), reads=[sn], writes=['w_out_bf'], acc=True)
            else:
                S.op('pool', lambda e, st=st, k=k: e.tensor_copy(out=w_out_bf[:, k, :], in_=st[:, 0:D]), reads=[sn], writes=['w_out_bf'], acc=True)
        cvec = sb("cvec", [NR, D], st=ph)
        S.op('dve', lambda e: e.memset(cvec[:], 0.0), writes=['cvec'])
        S.dma('sp', lambda e: e.dma_start(out=cvec[PR:PR + 1, :], in_=c_p), 'cvec', writes=['cvec'])
        S.dma('sp', lambda e: e.dma_start(out=cvec[0:SB, :], in_=c_s), 'cvec', writes=['cvec'])
        csil = sb("csil", [NR, D], st=ph)
        S.op('act', lambda e: e.activation(out=csil[:], in_=cvec[:], func=AF.Silu), reads=['cvec'], writes=['csil'])
        csT = sb("csT", [128, 8, NR], st=ph)
        for k in range(8):
            S.op('pe', lambda e, k=k: e.transpose(out=pm[:, k * 64:k * 64 + NR], in_=csil[:, k * 128:(k + 1) * 128], identity=ident_f[0:NR, 0:NR]),
                 reads=['csil', 'ident_f'], writes=['pm'], acc=True)
        S.op('dve', lambda e: e.tensor_copy(out=csT[:], in_=pm[:, 0:512].rearrange("p (k r) -> p k r", r=64)[:, :, 0:NR]), reads=['pm'], writes=['csT'])
        bada = sb("bada", [1, 3 * D], st=ph)
        S.dma('sp', lambda e: e.dma_start(out=bada[:], in_=b_ada), 'bada', writes=['bada'])
        wst = [sb("wst0", [128, 512], st=ph), sb("wst1", [128, 512], st=ph)]
        i = 0
        for j in range(6):
            pa = pacc[j % 2]
            pn = 'pacc%d' % (j % 2)
            for k in range(8):
                w_ = wst[i % 2]
                wn = 'wst%d' % (i % 2)
                S.dma('sp', lambda e, w_=w_, k=k, j=j: e.dma_start(out=w_[:], in_=w_ada[k * 128:(k + 1) * 128, j * 512:(j + 1) * 512]), wn, writes=[wn], acc=False)
                S.op('pe', lambda e, pa=pa, w_=w_, k=k: e.matmul(pa[0:NR, :], lhsT=csT[:, k, :], rhs=w_[:], start=(k == 0), stop=False),
                     reads=['csT', wn], writes=[pn], acc=(k > 0))
                i += 1
            S.op('pe', lambda e, pa=pa, j=j: e.matmul(pa[0:NR, :], lhsT=one1[0:1, 0:NR], rhs=bada[0:1, j * 512:(j + 1) * 512], start=False, stop=True),
                 reads=['one1', 'bada'], writes=[pn], acc=True)
            S.op('dve', lambda e, pa=pa, j=j: e.tensor_copy(out=mod_t[:, j * 512:(j + 1) * 512], in_=pa[0:NR, :]),
                 reads=[pn], writes=['mod_t'], acc=True)
        S.op('dve', lambda e: e.tensor_scalar_add(out=mod_t[:, D:2 * D], in0=mod_t[:, D:2 * D], scalar1=1.0), reads=['mod_t'], writes=['mod_t'])
        for j in range(24):
            S.op('pe', lambda e, j=j: e.matmul(pm[:, 256 + j:257 + j], lhsT=mod_t[PR:PR + 1, j * 128:(j + 1) * 128], rhs=one1[PR:PR + 1, 0:1], start=True, stop=True),
                 reads=['mod_t', 'one1'], writes=['pm'], acc=True)
        S.op('dve', lambda e: e.tensor_copy(out=modT[:], in_=pm[:, 256:280]), reads=['pm'], writes=['modT'])
        for c in range(2):
            S.op('pe', lambda e, c=c: e.matmul(pacc[c][:, :], lhsT=one1[PR:PR + 1, 0:128], rhs=mod_t[PR:PR + 1, 2 * D + c * 512:2 * D + (c + 1) * 512], start=True, stop=True),
                 reads=['one1', 'mod_t'], writes=['pacc%d' % c])
            S.op('dve', lambda e, c=c: e.tensor_copy(out=gate_bc[:, c * 512:(c + 1) * 512], in_=pacc[c][:, :]), reads=['pacc%d' % c], writes=['gate_bc'], acc=(c > 0))
        S.barrier()

    with ExitStack() as ph:
        xs = sb("xs", [SB, D], st=ph)
        S.dma('sp', lambda e: e.dma_start(out=xs[:], in_=x_s), 'xs', writes=['xs'])
        mvs = sb("mvs", [128, 4], st=ph)
        st6s = sb("st6s", [128, 2, 6], st=ph)
        mix_s = sb("mix_s", [SB, 1024], st=ph)
        phe = ExitStack()
        layer_norm_stats(xs, 'xs', SB, mvs, st6s, 's')
        hs = sb("hs", [SB, D], st=phe)
        hs_bf = sb("hs_bf", [SB, D], BF16, st=phe)
        S.op('dve', lambda e: e.tensor_scalar(out=hs[:], in0=xs[:], scalar1=mvs[0:SB, 0:1], scalar2=mvs[0:SB, 2:3], op0=ALU.subtract, op1=ALU.mult),
             reads=['xs', 'mvs'], writes=['hs'])
        S.op('dve', lambda e: e.tensor_tensor(out=hs[:], in0=hs[:], in1=mod_t[0:SB, D:2 * D], op=ALU.mult), reads=['hs', 'mod_t'], writes=['hs'])
        S.op('dve', lambda e: e.tensor_tensor(out=hs_bf[:], in0=hs[:], in1=mod_t[0:SB, 0:D], op=ALU.add), reads=['hs', 'mod_t'], writes=['hs_bf'])
        hsT = sb("hsT", [128, 8, SB], BF16, st=phe)
        for k in range(8):
            S.op('pe', lambda e, k=k: e.transpose(out=pT[:, k * 32:k * 32 + SB], in_=hs_bf[:, k * 128:(k + 1) * 128], identity=ident[0:SB, 0:SB]),
                 reads=['hs_bf', 'ident'], writes=['pT'], acc=True)
        S.op('dve', lambda e: e.tensor_copy(out=hsT[:], in_=pT[:, 0:256].rearrange("p (k r) -> p k r", r=32)[:, :, 0:SB]), reads=['pT'], writes=['hsT'])
        in_proj(SB, lambda k: hsT[:, k, :], 'hsT', proj_s, 'proj_s', 'dve')
        kvs = sb("kvs", [SB, 768], st=phe)
        for br in range(3):
            src = proj_s[:, O_CK + br * 256:O_CK + (br + 1) * 256].rearrange("p (c k d) -> p k c d", c=2, k=2)
            dst = kvs[:, br * 256:(br + 1) * 256].rearrange("p (k c d) -> p k c d", k=2, c=2)
            S.op('dve', lambda e, src=src, dst=dst: e.tensor_copy(out=dst, in_=src), reads=['proj_s'], writes=['kvs'], acc=True)
        S.dma('sp', lambda e: e.dma_start(out=o_cmp_s, in_=kvs[:, 0:256]), 'kvs', reads=['kvs'])
        S.dma('sp', lambda e: e.dma_start(out=o_slc_s, in_=kvs[:, 256:512]), 'kvs', reads=['kvs'])
        S.dma('sp', lambda e: e.dma_start(out=o_win_s[:, 511, :], in_=kvs[:, 512:768]), 'kvs', reads=['kvs'])

        S.barrier()
        phe.close()
        phm = ExitStack()
        KS = 128.0 ** -0.5
        Cs = sb("Cs", [128, SB * 4, 128], st=phm)
        S.dma('sp', lambda e: e.dma_start(out=Cs[:], in_=st_C.rearrange("s h d e -> d (s h) e")), 'Cs', writes=['Cs'])
        ns = sb("ns", [SB, 512], st=phm)
        S.dma('sp', lambda e: e.dma_start(out=ns[:], in_=st_n), 'ns', writes=['ns'])
        sg = sb("sg", [SB, 64], st=phm)
        S.dma('sp', lambda e: e.dma_start(out=sg[:, 0:4], in_=st_m), 'sg', writes=['sg'])
        S.op('act', lambda e: e.activation(out=sg[:, 4:8], in_=proj_s[:, O_FM:O_FM + 4], func=AF.Exp, scale=-1.0), reads=['proj_s'], writes=['sg4'])
        S.op('act', lambda e: e.activation(out=sg[:, 4:8], in_=sg[:, 4:8], func=AF.Ln, bias=onesf[0:SB, 0:1], scale=1.0), reads=['sg4', 'onesf'], writes=['sg4'])
        S.op('dve', lambda e: e.tensor_tensor(out=sg[:, 8:12], in0=sg[:, 0:4], in1=sg[:, 4:8], op=ALU.subtract), reads=['sg', 'sg4'], writes=['sg8'])
        S.op('dve', lambda e: e.tensor_tensor(out=sg[:, 12:16], in0=sg[:, 8:12], in1=proj_s[:, O_IM:O_IM + 4], op=ALU.max), reads=['sg8', 'proj_s'], writes=['sg12'])
        S.dma('sp', lambda e: e.dma_start(out=o_m_s, in_=sg[:, 12:16]), 'sg12', reads=['sg12'])
        S.op('dve', lambda e: e.tensor_tensor(out=sg[:, 16:20], in0=sg[:, 8:12], in1=sg[:, 12:16], op=ALU.subtract), reads=['sg8', 'sg12'], writes=['sg16'])
        S.op('dve', lambda e: e.tensor_tensor(out=sg[:, 20:24], in0=proj_s[:, O_IM:O_IM + 4], in1=sg[:, 12:16], op=ALU.subtract), reads=['proj_s', 'sg12'], writes=['sg16'], acc=True)
        S.op('act', lambda e: e.activation(out=sg[:, 16:24], in_=sg[:, 16:24], func=AF.Exp), reads=['sg16'], writes=['sg16'])
        S.op('act', lambda e: e.activation(out=sg[:, 24:28], in_=sg[:, 12:16], func=AF.Exp, scale=-1.0), reads=['sg12'], writes=['sg24'])
        w_inter = sg[:, 16:20]
        w_inp = sg[:, 20:24]
        emn = sg[:, 24:28]
        kw = sb("kw", [SB, 512], st=phm)
        S.op('dve', lambda e: e.scalar_tensor_tensor(out=kw[:].rearrange("p (h d) -> p h d", h=4), in0=proj_s[:, O_KM:O_KM + 512].rearrange("p (h d) -> p h d", h=4),
                                                     scalar=KS, in1=w_inp.unsqueeze(2).to_broadcast([SB, 4, 128]), op0=ALU.mult, op1=ALU.mult),
             reads=['proj_s', 'sg16'], writes=['kw'])
        S.op('dve', lambda e: e.tensor_tensor(out=ns[:].rearrange("p (h d) -> p h d", h=4), in0=ns[:].rearrange("p (h d) -> p h d", h=4),
                                              in1=w_inter.unsqueeze(2).to_broadcast([SB, 4, 128]), op=ALU.mult), reads=['ns', 'sg16'], writes=['ns'])
        S.op('dve', lambda e: e.tensor_tensor(out=ns[:], in0=ns[:], in1=kw[:], op=ALU.add), reads=['ns', 'kw'], writes=['ns'])
        S.dma('sp', lambda e: e.dma_start(out=o_n_s, in_=ns[:]), 'ns_o', reads=['ns'])
        Wd = sb("Wd", [SB, SB, 4], st=phm)
        S.op('dve', lambda e: e.tensor_tensor(out=Wd[:], in0=ident_f[0:SB, 0:SB].unsqueeze(2).to_broadcast([SB, SB, 4]),
                                              in1=w_inter.unsqueeze(1).to_broadcast([SB, SB, 4]), op=ALU.mult), reads=['ident_f', 'sg16'], writes=['Wd'])
        S.op('pe', lambda e: e.matmul(pm[:, 0:64], lhsT=onesf[0:SB, :], rhs=Wd[:].rearrange("p a b -> p (a b)"), start=True, stop=True), reads=['onesf', 'Wd'], writes=['pm'])
        wib = sb("wib", [128, 64], st=phm)
        S.op('dve', lambda e: e.tensor_copy(out=wib[:], in_=pm[:, 0:64]), reads=['pm'], writes=['wib'])
        kwm = [sb("kwm0", [SB, 512], st=phm), sb("kwm1", [SB, 512], st=phm)]
        for s in range(SB):
            km_ = kwm[s % 2]
            kn = 'kwm%d' % (s % 2)
            pb = pacc[s % 2]
            pbn = 'pacc%d' % (s % 2)
            S.op('dve', lambda e, km_=km_, s=s: e.tensor_scalar(out=km_[:], in0=kw[:], scalar1=ident_f[0:SB, s:s + 1], scalar2=None, op0=ALU.mult),
                 reads=['kw', 'ident_f'], writes=[kn])
            for h in range(4):
                S.op('pe', lambda e, km_=km_, pb=pb, h=h: e.matmul(pb[:, h * 128:(h + 1) * 128], lhsT=km_[:, h * 128:(h + 1) * 128],
                                                                  rhs=proj_s[:, O_VM + h * 128:O_VM + (h + 1) * 128], start=True, stop=True),
                     reads=[kn, 'proj_s'], writes=[pbn], acc=(h > 0))
            S.op('pool', lambda e, s=s: e.tensor_tensor(out=Cs[:, s * 4:(s + 1) * 4, :], in0=Cs[:, s * 4:(s + 1) * 4, :],
                                                        in1=wib[:, s * 4:(s + 1) * 4].unsqueeze(2).to_broadcast([128, 4, 128]), op=ALU.mult),
                 reads=['Cs', 'wib'], writes=['Cs'])
            S.op('dve', lambda e, s=s, pb=pb: e.tensor_tensor(out=Cs[:, s * 4:(s + 1) * 4, :], in0=Cs[:, s * 4:(s + 1) * 4, :],
                                                       in1=pb[:].rearrange("p (h e) -> p h e", h=4), op=ALU.add),
                 reads=['Cs', pbn], writes=['Cs'])
        S.dma('sp', lambda e: e.dma_start(out=o_C_s.rearrange("s h d e -> d (s h) e"), in_=Cs[:]), 'Cs_o', reads=['Cs'])
        qTs = sb("qTs", [128, 4, SB], st=phm)
        for h in range(4):
            S.op('pe', lambda e, h=h: e.transpose(out=pm[:, 64 + h * 16:64 + (h + 1) * 16], in_=proj_s[:, O_QM + h * 128:O_QM + (h + 1) * 128], identity=ident_f[0:SB, 0:SB]),
                 reads=['proj_s', 'ident_f', 'wib'], writes=['pm'], acc=True)
        S.op('dve', lambda e: e.tensor_copy(out=qTs[:], in_=pm[:, 64:128].rearrange("p (h s) -> p h s", h=4)), reads=['pm'], writes=['qTs'])
        qTm = sb("qTm", [128, 4, SB, SB], st=phm)
        for h in range(4):
            S.op('dve', lambda e, h=h: e.tensor_tensor(out=qTm[:, h, :, :], in0=qTs[:, h, :].unsqueeze(1).to_broadcast([128, SB, SB]), in1=i16bc, op=ALU.mult),
                 reads=['qTs', 'ident_f'], writes=['qTm'], acc=(h > 0))
        for h in range(4):
            for s in range(SB):
                S.op('pe', lambda e, h=h, s=s: e.matmul(pS[0:SB, h * 128:(h + 1) * 128], lhsT=qTm[:, h, s, :], rhs=Cs[:, s * 4 + h, :], start=(s == 0), stop=(s == SB - 1)),
                     reads=['qTm', 'Cs'], writes=['pS'], acc=(h > 0 or s > 0))
        qn = sb("qn", [SB, 512], st=phm)
        S.op('dve', lambda e: e.tensor_tensor(out=qn[:], in0=proj_s[:, O_QM:O_QM + 512], in1=ns[:], op=ALU.mult), reads=['proj_s', 'ns'], writes=['qn'])
        S.op('dve', lambda e: e.tensor_reduce(out=sg[:, 28:32], in_=qn[:].rearrange("p (h d) -> p h d", h=4), axis=AX.X, op=ALU.add), reads=['qn'], writes=['sg28'])
        S.op('act', lambda e: e.activation(out=sg[:, 28:32], in_=sg[:, 28:32], func=AF.Abs), reads=['sg28'], writes=['sg28'])
        S.op('dve', lambda e: e.tensor_tensor(out=sg[:, 28:32], in0=sg[:, 28:32], in1=emn, op=ALU.max), reads=['sg28', 'sg24'], writes=['sg28'])
        S.op('dve', lambda e: e.reciprocal(out=sg[:, 28:32], in_=sg[:, 28:32]), reads=['sg28'], writes=['sg28'])
        hsr = sb("hsr", [SB, 4, 128], st=phm)
        S.op('dve', lambda e: e.tensor_tensor(out=hsr[:], in0=pS[0:SB, :].rearrange("p (h e) -> p h e", h=4), in1=sg[:, 28:32].unsqueeze(2).to_broadcast([SB, 4, 128]), op=ALU.mult),
             reads=['pS', 'sg28'], writes=['hsr'])
        st6s2 = sb("st6s2", [SB, 4, 6], st=phm)
        mvs2 = sb("mvs2", [SB, 4, 2], st=phm)
        rss = sb("rss", [SB, 4], st=phm)
        for h in range(4):
            S.op('dve', lambda e, h=h: e.bn_stats(out=st6s2[:, h, :], in_=hsr[:, h, :]), reads=['hsr'], writes=['st6s2'], acc=(h > 0))
        for h in range(4):
            S.op('dve', lambda e, h=h: e.bn_aggr(out=mvs2[:, h, :], in_=st6s2[:, h, :]), reads=['st6s2'], writes=['mvs2'], acc=(h > 0))
        S.op('act', lambda e: e.activation(out=rss[:], in_=mvs2[:, :, 1], func=AF.Sqrt, bias=epsc[0:SB, :], scale=1.0), reads=['mvs2', 'epsc'], writes=['rss'])
        S.op('dve', lambda e: e.reciprocal(out=rss[:], in_=rss[:]), reads=['rss'], writes=['rss'])
        S.op('dve', lambda e: e.tensor_tensor(out=hsr[:], in0=hsr[:], in1=mvs2[:, :, 0:1].to_broadcast([SB, 4, 128]), op=ALU.subtract), reads=['hsr', 'mvs2'], writes=['hsr'])
        S.op('dve', lambda e: e.tensor_tensor(out=hsr[:], in0=hsr[:], in1=rss[:].unsqueeze(2).to_broadcast([SB, 4, 128]), op=ALU.mult), reads=['hsr', 'rss'], writes=['hsr'])
        hsf = hsr[:].rearrange("p h d -> p (h d)")
        S.op('dve', lambda e: e.tensor_tensor(out=hsf, in0=hsf, in1=ng_bc[0:SB, :], op=ALU.mult), reads=['hsr', 'ng_bc'], writes=['hsr'])
        sigs = sb("sigs", [SB, 1024], st=phm)
        S.op('act', lambda e: e.activation(out=sigs[:], in_=proj_s[:, O_OM:O_OM + 1024], func=AF.Sigmoid), reads=['proj_s'], writes=['sigs'])
        S.op('dve', lambda e: e.tensor_tensor(out=sigs[:, 0:512], in0=sigs[:, 0:512], in1=sigs[:, 512:1024], op=ALU.mult), reads=['sigs'], writes=['sigs'])
        S.op('dve', lambda e: e.tensor_tensor(out=sigs[:, 0:512], in0=sigs[:, 0:512], in1=proj_s[:, O_ZM:O_ZM + 512], op=ALU.mult), reads=['sigs', 'proj_s'], writes=['sigs'])
        S.op('dve', lambda e: e.tensor_tensor(out=mix_s[:, 0:512], in0=hsf, in1=sigs[:, 0:512], op=ALU.mult), reads=['hsr', 'sigs'], writes=['mix_s'])
        S.barrier()
        phm.close()
        phw = ExitStack()
        wcp = [sb("wcp0", [128, 4, 256], st=phw), sb("wcp1", [128, 4, 256], st=phw)]
        for s in range(SB):
            wc = wcp[s % 2]
            wn = 'wcp%d' % (s % 2)
            S.dma('sp', lambda e, s=s, wc=wc: e.dma_start(out=wc[:, 0:3, :], in_=win_c[s, 1:385, :].rearrange("(j p) f -> p j f", p=128)), wn, writes=[wn], acc=False)
            S.dma('sp', lambda e, s=s, wc=wc: e.dma_start(out=wc[0:127, 3, :], in_=win_c[s, 385:512, :]), wn, writes=[wn], acc=True)
            S.dma('sp', lambda e, s=s, wc=wc: e.dma_start(out=o_win_s[s, 0:384, :].rearrange("(j p) f -> p j f", p=128), in_=wc[:, 0:3, :]), wn, reads=[wn])
            S.dma('sp', lambda e, s=s, wc=wc: e.dma_start(out=o_win_s[s, 384:511, :], in_=wc[0:127, 3, :]), wn, reads=[wn])
        S.barrier()
        phw.close()

        cs = sb("cs", [128, 549], st=ph)
        S.dma('sp', lambda e: e.dma_start(out=cs[:], in_=consts_s_d[:, 0:549]), 'cs', writes=['cs'])
        selq = cs[:, 0:32]
        pmod4 = cs[:, 32:33]
        cover_s = cs[:, 33:549].rearrange("p (c b) -> p c b", c=4)
        rb_s = sb("rb_s", [128, 32, 8], st=ph)
        S.dma('sp', lambda e: e.dma_start(out=rb_s[:].rearrange("p b h -> p (b h)"), in_=rel_bias.partition_broadcast(128)), 'rb_s', writes=['rb_s'])
        rbd_s = sb("rbd_s", [128, 32, 8], st=ph)
        S.op('dve', lambda e: e.tensor_tensor(out=rbd_s[:], in0=rb_s[:], in1=rb_s[:, 31:32, :].to_broadcast([128, 32, 8]), op=ALU.subtract), reads=['rb_s'], writes=['rbd_s'])
        rb32 = sb("rb32", [32, 8], st=ph)
        S.dma('sp', lambda e: e.dma_start(out=rb32[:], in_=rel_bias.rearrange("o (b h) -> (o b) h", h=8)), 'rb32', writes=['rb32'])
        ohcs = sb("ohcs", [32, 512], st=ph)
        S.dma('sp', lambda e: e.dma_start(out=ohcs[:], in_=ohcs_d), 'ohcs', writes=['ohcs'])
        bcs = sb("bcs", [4, 2, 512], st=ph)
        for kv in range(2):
            S.op('pe', lambda e, kv=kv: e.matmul(pm[0:4, 0:511], lhsT=rb32[:, kv * 4:(kv + 1) * 4], rhs=ohcs[:, 0:511], start=True, stop=True), reads=['rb32', 'ohcs'], writes=['pm'])
            S.op('dve', lambda e, kv=kv: e.tensor_copy(out=bcs[:, kv, 0:511], in_=pm[0:4, 0:511]), reads=['pm'], writes=['bcs'], acc=(kv > 0))
        Bn = sb("Bn", [128, 32, 8], st=ph)
        Bw = sb("Bw", [128, 4, 8], st=ph)
        cbs = sb("cbs", [64, 2], st=ph)
        w2s = sb("w2s", [64, 2, 128], BF16, st=ph)
        idx_i = sb("idx_i", [128, 32], I32, st=ph)
        W1bd = sb("W1bd", [128, 32, 128], BF16, st=ph)
        cbs2 = sb("cbs2", [128, 1], st=ph)
        w2s2 = sb("w2s2", [128, 64], BF16, st=ph)
        with ExitStack() as ph2:
            W1s = sb("W1s", [64, 2, 32, 64], BF16, st=ph2)
            tmpn = sb("tmpn", [128, 32, 32], st=ph2)
            cs2 = sb("cs2", [128, 1152], st=ph2)
            S.dma('sp', lambda e: e.dma_start(out=cs2[:], in_=consts_s_d[:, 549:1701]), 'cs2', writes=['cs'])
            ohn = cs2[:, 0:1024].rearrange("p (b r) -> p b r", b=32)
            ohw = cs2[:, 1024:1152].rearrange("p (b r) -> p b r", b=32)
            for h in range(8):
                S.op('dve', lambda e, h=h: e.tensor_tensor(out=tmpn[:], in0=ohn, in1=rbd_s[:, :, h:h + 1].to_broadcast([128, 32, 32]), op=ALU.mult), reads=['cs', 'rbd_s'], writes=['tmpn'])
                S.op('dve', lambda e, h=h: e.tensor_reduce(out=Bn[:, :, h], in_=tmpn[:].rearrange("p b r -> p r b"), axis=AX.X, op=ALU.add), reads=['tmpn'], writes=['Bn'], acc=True)
                S.op('dve', lambda e, h=h: e.tensor_tensor(out=tmpn[:, :, 0:4], in0=ohw, in1=rbd_s[:, :, h:h + 1].to_broadcast([128, 32, 4]), op=ALU.mult), reads=['cs', 'rbd_s', 'Bn'], writes=['tmpn'])
                S.op('dve', lambda e, h=h: e.tensor_reduce(out=Bw[:, :, h], in_=tmpn[:, :, 0:4].rearrange("p b r -> p r b"), axis=AX.X, op=ALU.add), reads=['tmpn'], writes=['Bw'], acc=True)
            load_cmp_weights(ph2, W1s, cbs, w2s, '_s')
            st2 = sb("st2", [128, 32, 64], st=ph2)
            for c in range(2):
                S.dma('sp', lambda e, c=c: e.dma_start(out=st2[c * 64:(c + 1) * 64, :, :], in_=cmp_w1[c].rearrange("p d h -> d p h")), 'st2', writes=['st2'])
            S.op('pool', lambda e: e.memset(W1bd[:], 0.0), writes=['W1bd'])
            S.op('dve', lambda e: e.tensor_copy(out=W1bd[0:64, :, 0:64], in_=st2[0:64, :, :]), reads=['st2', 'W1bd'], writes=['W1bd'], acc=True)
            S.op('dve', lambda e: e.tensor_copy(out=W1bd[64:128, :, 64:128], in_=st2[64:128, :, :]), reads=['st2', 'W1bd'], writes=['W1bd'], acc=True)
            S.op('dve', lambda e: e.tensor_copy(out=cbs2[0:64, :], in_=cbs[:, 0:1]), reads=['cbs'], writes=['cbs2'])
            S.dma('sp', lambda e: e.dma_start(out=cbs2[64:128, :], in_=cbs[:, 1:2]), 'cbs2', reads=['cbs'], writes=['cbs2'])
            w2f = sb("w2f", [128, 64], st=ph2)
            S.dma('sp', lambda e: e.dma_start(out=w2f[:], in_=cmp_w2.rearrange("c h e -> (c h) e")), 'w2f', writes=['w2f'])
            S.op('dve', lambda e: e.tensor_copy(out=w2s2[:], in_=w2f[:]), reads=['w2f'], writes=['w2s2'])
            pt_i = sb("pt_i", [128, SB * 64], I32, st=ph2)
            S.dma('sp', lambda e: e.dma_start(out=pt_i[:], in_=page_tab.partition_broadcast(128)), 'pt_i', writes=['pt_i'])
            pt_f = sb("pt_f", [128, 32, 32], st=ph2)
            S.op('dve', lambda e: e.tensor_copy(out=pt_f[:].rearrange("p a k -> p (a k)"), in_=pt_i[:]), reads=['pt_i'], writes=['pt_f'])
            S.op('dve', lambda e: e.tensor_tensor(out=pt_f[:], in0=pt_f[:], in1=selq.unsqueeze(1).to_broadcast([128, 32, 32]), op=ALU.mult), reads=['pt_f', 'cs'], writes=['pt_f'])
            idx_f = sb("idx_f", [128, 32], st=ph2)
            S.op('dve', lambda e: e.tensor_reduce(out=idx_f[:], in_=pt_f[:], axis=AX.X, op=ALU.add), reads=['pt_f'], writes=['idx_f'])
            S.op('dve', lambda e: e.tensor_scalar(out=idx_f[:], in0=idx_f[:], scalar1=4.0, scalar2=pmod4, op0=ALU.mult, op1=ALU.add), reads=['idx_f', 'cs'], writes=['idx_f'])
            S.op('dve', lambda e: e.tensor_copy(out=idx_i[:], in_=idx_f[:]), reads=['idx_f'], writes=['idx_i'])
            S.barrier()
        qs_bf = sb("qs_bf", [SB, 512], BF16, st=ph)
        S.op('act', lambda e: e.mul(out=qs_bf[:], in_=proj_s[:, O_QA:O_QA + 512], mul=ATT_SCALE), reads=['proj_s'], writes=['qs_bf'])
        qsT = sb("qsT", [64, SB, 8], BF16, st=ph)
        for h in range(8):
            S.op('pe', lambda e, h=h: e.transpose(out=pT[0:64, h * 16:(h + 1) * 16], in_=qs_bf[:, h * 64:(h + 1) * 64], identity=ident[0:SB, 0:SB]),
                 reads=['qs_bf', 'ident'], writes=['pT'], acc=(h > 0))
        S.op('dve', lambda e: e.tensor_copy(out=qsT[:].rearrange("d s h -> d h s"), in_=pT[0:64, 0:128].rearrange("d (h s) -> d h s", h=8)), reads=['pT'], writes=['qsT'])
        pnm = [None, sb("pnm1", [SB, SB, 8], BF16, st=ph), sb("pnm2", [SB, SB, 8], BF16, st=ph)]
        vnew = [None, sb("vnew1", [SB, 128], BF16, st=ph), sb("vnew2", [SB, 128], BF16, st=ph)]
        prod = sb("prod", [SB, 8, 64], st=ph)
        sn = sb("sn", [SB, 8], st=ph)
        for br, ok_, ov_ in ((1, O_SK, O_SV), (2, O_WK, O_WV)):
            S.op('dve', lambda e, ok_=ok_: e.tensor_tensor(out=prod[:].rearrange("p (k g) d -> p k g d", k=2), in0=proj_s[:, O_QA:O_QA + 512].rearrange("p (k g d) -> p k g d", k=2, g=4),
                                                         in1=proj_s[:, ok_:ok_ + 128].rearrange("p (k d) -> p k d", k=2).unsqueeze(2).to_broadcast([SB, 2, 4, 64]), op=ALU.mult),
                 reads=['proj_s', 'sn'], writes=['prod'])
            S.op('dve', lambda e: e.tensor_reduce(out=sn[:], in_=prod[:], axis=AX.X, op=ALU.add), reads=['prod'], writes=['sn'])
            S.op('dve', lambda e: e.scalar_tensor_tensor(out=sn[:], in0=sn[:], scalar=ATT_SCALE, in1=rbd_s[0:SB, 0, :], op0=ALU.mult, op1=ALU.add), reads=['sn', 'rbd_s'], writes=['sn'])
            S.op('act', lambda e: e.activation(out=sn[:], in_=sn[:], func=AF.Exp), reads=['sn'], writes=['sn'])
            S.op('dve', lambda e, br=br: e.tensor_tensor(out=pnm[br][:], in0=sn[:].unsqueeze(1).to_broadcast([SB, SB, 8]), in1=ident_f[0:SB, 0:SB].unsqueeze(2).to_broadcast([SB, SB, 8]), op=ALU.mult),
                 reads=['sn', 'ident_f'], writes=['pnm%d' % br])
            S.op('dve', lambda e, br=br, ov_=ov_: e.tensor_copy(out=vnew[br][:], in_=proj_s[:, ov_:ov_ + 128]), reads=['proj_s'], writes=['vnew%d' % br])
        phn = ExitStack()
        raw = [sb("raw0", [128, 8192], st=phn)]
        rawbf = sb("rawbf", [128, 2, 8192], BF16, st=phn)
        cTr = [sb("cTr0", [128, 8, 128], BF16, st=phn), sb("cTr1", [128, 8, 128], BF16, st=phn)]
        A1sb = sb("A1sb", [128, 512], st=phn)
        pres = sb("pres", [128, 512], st=phn)
        gxs = sb("gxs", [128, 512], st=phn)
        gs_ = sb("gs_", [128, 2, 512], BF16, st=phn)
        KcTs = sb("KcTs", [64, 512], BF16, st=phn)
        Vcs = sb("Vcs", [128, 4, 64], BF16, st=phn)
        scs = sb("scs", [4, 512], st=phn)
        zs = sb("zs", [4, 8], st=phn)
        pTf = sb("pTf", [128, 4, 4], st=phn)
        pTb = sb("pTb", [128, 4, 4], BF16, st=phn)
        s4 = sb("s4", [4, 136], st=phn)
        sco1 = sb("sco1", [1, 136], st=phn)
        wk1 = sb("wk1", [1, 136], st=phn)
        m81 = sb("m81", [1, 16], st=phn)
        ndup = sb("ndup", [1, 128, 2], st=phn)
        mcol = sb("mcol", [128, 4], st=phn)
        ssb = sb("ssb", [128, 32, 4], st=phn)
        PTs = sb("PTs", [128, 32, 4], BF16, st=phn)
        zp = sb("zp", [128, 2, 2, 4], st=phn)
        osb = [sb("osb0", [4, 2, 65], st=phn), sb("osb1", [4, 2, 65], st=phn)]
        S.op('dve', lambda e: e.memset(osb[0][:], 1.0), writes=['osb0'])
        S.op('dve', lambda e: e.memset(osb[1][:], 1.0), writes=['osb1'])
        raww = raw[0][:, 0:1024].rearrange("p (r f) -> p r f", r=4)
        rawwbf = rawbf[:, 0, 0:1024].rearrange("p (r f) -> p r f", r=4)
        PTw = sb("PTw", [128, 4, 4], BF16, st=phn)
        gcnt = [0]
        ocnt = [0]

        def gather(cache, s, hs):
            i = gcnt[0]
            gcnt[0] += 1
            rw = raw[0]
            rn = 'raw0'
            col = s * 2 + hs
            S.dma('pool', lambda e, rw=rw, col=col, cache=cache: e.indirect_dma_start(out=rw[:], out_offset=None, in_=cache,
                                                                                     in_offset=bass.IndirectOffsetOnAxis(ap=idx_i[:, col:col + 1], axis=0)),
                  rn, reads=['idx_i'], writes=[rn], acc=False)
            S.op('act', lambda e, rw=rw, hs=hs: e.copy(out=rawbf[:, hs, 0:2816], in_=rw[:, 0:2816]), reads=[rn], writes=['rawbf%d' % hs])
            S.op('dve', lambda e, rw=rw, hs=hs: e.tensor_copy(out=rawbf[:, hs, 2816:5632], in_=rw[:, 2816:5632]), reads=[rn], writes=['rawbf%d' % hs], acc=True)
            S.op('pool', lambda e, rw=rw, hs=hs: e.tensor_copy(out=rawbf[:, hs, 5632:8192], in_=rw[:, 5632:8192]), reads=[rn], writes=['rawbf%d' % hs], acc=True)

        def store_o(br, s, pO_list):
            i = ocnt[0]
            ocnt[0] += 1
            ob = osb[i % 2]
            on = 'osb%d' % (i % 2)
            for kv, (pO, pOn, w) in enumerate(pO_list):
                S.op('dve', lambda e, ob=ob, kv=kv, pO=pO, w=w: e.tensor_copy(out=ob[:, kv, 0:w], in_=pO[0:4, 0:w]), reads=[pOn], writes=[on], acc=(kv > 0))
                if w == 64:
                    S.op('dve', lambda e, ob=ob, kv=kv: e.memset(ob[:, kv, 64:65], 1.0), reads=[on], writes=[on], acc=True)
            S.dma('sp', lambda e, ob=ob, br=br, s=s: e.dma_start(out=o_scr[br, s].rearrange("k g e -> g k e"), in_=ob[:]), on, reads=[on], writes=['o_scr'])

        tcnt = [0]
        for s in range(SB):
            gather(cache_cmp, s, 0)
            gather(cache_cmp, s, 1)
            for kv in range(2):
                for hs in range(2):
                    for rg in range(4):
                        ct = cTr[tcnt[0] % 2]
                        ctn = 'cTr%d' % (tcnt[0] % 2)
                        tcnt[0] += 1
                        for rr in range(8):
                            r = rg * 8 + rr
                            S.op('pe', lambda e, rr=rr, r=r, hs=hs, kv=kv: e.transpose(out=pT[:, rr * 128:(rr + 1) * 128], in_=rawbf[:, hs, r * 256 + kv * 128:r * 256 + kv * 128 + 128], identity=ident[:]),
                                 reads=['rawbf%d' % hs, 'ident'], writes=['pT'], acc=(rr > 0))
                        if tcnt[0] % 2:
                            S.op('dve', lambda e, ct=ct: e.tensor_copy(out=ct[:], in_=pT[:, :].rearrange("p (a t) -> p a t", a=8)), reads=['pT'], writes=[ctn])
                        else:
                            S.op('act', lambda e, ct=ct: e.activation(out=ct[:], in_=pT[:, :].rearrange("p (a t) -> p a t", a=8), func=AF.Identity), reads=['pT'], writes=[ctn])
                        par = rg // 2
                        for j, (pA, pAn) in enumerate(((pS, 'pS'), (pN, 'pN'))):
                            for rr in range(8):
                                r = rg * 8 + rr
                                pp = r % 16
                                S.op('pe', lambda e, pA=pA, ct=ct, rr=rr, j=j, pp=pp, hs=hs, par=par: e.matmul(pA[:, hs * 256 + par:hs * 256 + 256:2], lhsT=W1bd[:, j * 16 + pp, :], rhs=ct[:, rr, :],
                                                                                                          start=(pp == 0), stop=(pp == 15)),
                                     reads=['W1bd', ctn], writes=[pAn], acc=True)
                S.op('act', lambda e: e.copy(out=A1sb[:], in_=pN[:, :]), reads=['pN'], writes=['A1sb'])
                S.op('dve', lambda e: e.tensor_tensor(out=pres[:, 0:511], in0=pS[:, 0:511], in1=A1sb[:, 1:512], op=ALU.add), reads=['pS', 'A1sb'], writes=['pres'])
                S.op('act', lambda e: e.activation(out=gxs[:, 0:511], in_=pres[:, 0:511], func=AF.Identity, bias=cbs2[:, 0:1], scale=1.0), reads=['pres', 'cbs2'], writes=['gxs'])
                S.op('dve', lambda e: e.tensor_tensor(out=pres[:, 0:511], in0=gxs[:, 0:511], in1=gxs[:, 0:511], op=ALU.mult), reads=['gxs', 'pres'], writes=['pres'])
                S.op('dve', lambda e: e.tensor_scalar(out=pres[:, 0:511], in0=pres[:, 0:511], scalar1=0.044715, scalar2=1.0, op0=ALU.mult, op1=ALU.add), reads=['pres'], writes=['pres'])
                S.op('dve', lambda e: e.tensor_tensor(out=pres[:, 0:511], in0=pres[:, 0:511], in1=gxs[:, 0:511], op=ALU.mult), reads=['pres', 'gxs'], writes=['pres'])
                S.op('act', lambda e: e.activation(out=pres[:, 0:511], in_=pres[:, 0:511], func=AF.Sigmoid, scale=1.5957691216), reads=['pres'], writes=['pres'])
                S.op('dve', lambda e, kv=kv: e.tensor_tensor(out=gs_[:, kv, 0:511], in0=pres[:, 0:511], in1=gxs[:, 0:511], op=ALU.mult), reads=['pres', 'gxs'], writes=['gs_'], acc=True)
            NCH = [(0, 128), (1, 128), (2, 128), (3, 127)]
            for kv in range(2):
                S.op('pe', lambda e, kv=kv: e.matmul(pm[0:64, 0:511], lhsT=w2s2[0:64, :], rhs=gs_[0:64, kv, 0:511], start=True, stop=True), reads=['w2s2', 'gs_'], writes=['pm'])
                S.op('act', lambda e: e.copy(out=KcTs[:, 0:511], in_=pm[0:64, 0:511]), reads=['pm'], writes=['KcTs'])
                for ch, nch in NCH:
                    S.op('pe', lambda e, kv=kv, ch=ch, nch=nch: e.matmul(pC[0:nch, ch * 64:(ch + 1) * 64], lhsT=gs_[64:128, kv, ch * 128:ch * 128 + nch], rhs=w2s2[64:128, :], start=True, stop=True),
                         reads=['w2s2', 'gs_'], writes=['pC'], acc=(ch > 0))
                S.op('dve', lambda e: e.tensor_copy(out=Vcs[:, 0:3, :], in_=pC[:, 0:192].rearrange("p (c e) -> p c e", c=3)), reads=['pC'], writes=['Vcs'])
                S.op('dve', lambda e: e.tensor_copy(out=Vcs[0:127, 3, :], in_=pC[0:127, 192:256]), reads=['pC'], writes=['Vcs'], acc=True)
                S.op('pe', lambda e, kv=kv, s=s: e.matmul(pm[0:4, 0:511], lhsT=qsT[:, s, kv * 4:(kv + 1) * 4], rhs=KcTs[:, 0:511], start=True, stop=True), reads=['qsT', 'KcTs'], writes=['pm'])
                S.op('dve', lambda e, kv=kv: e.tensor_tensor(out=scs[:, 0:511], in0=pm[0:4, 0:511], in1=bcs[:, kv, 0:511], op=ALU.add), reads=['pm', 'bcs'], writes=['scs'])
                S.op('act', lambda e: e.activation(out=scs[:, 0:511], in_=scs[:, 0:511], func=AF.Exp), reads=['scs'], writes=['scs'])
                S.op('dve', lambda e: e.tensor_reduce(out=zs[:, 0:1], in_=scs[:, 0:511], axis=AX.X, op=ALU.add), reads=['scs'], writes=['zs'])
                S.op('dve', lambda e: e.reciprocal(out=zs[:, 0:1], in_=zs[:, 0:1]), reads=['zs'], writes=['zs'])
                S.op('dve', lambda e: e.tensor_scalar(out=scs[:, 0:511], in0=scs[:, 0:511], scalar1=zs[:, 0:1], scalar2=None, op0=ALU.mult), reads=['scs', 'zs'], writes=['scs'])
                for ch, nch in NCH:
                    S.op('pe', lambda e, ch=ch, nch=nch: e.transpose(out=pm[0:nch, ch * 4:(ch + 1) * 4], in_=scs[:, ch * 128:ch * 128 + nch], identity=ident_f[0:4, 0:4]),
                         reads=['scs', 'ident_f'], writes=['pm'], acc=(ch > 0))
                S.op('dve', lambda e: e.tensor_copy(out=pTf[:, 0:3, :], in_=pm[:, 0:12].rearrange("p (c g) -> p c g", c=3)), reads=['pm'], writes=['pTf'])
                S.op('dve', lambda e: e.tensor_copy(out=pTf[0:127, 3, :], in_=pm[0:127, 12:16]), reads=['pm'], writes=['pTf'], acc=True)
                S.op('dve', lambda e: e.tensor_copy(out=pTb[:, 0:3, :], in_=pTf[:, 0:3, :]), reads=['pTf'], writes=['pTb'])
                S.op('dve', lambda e: e.tensor_copy(out=pTb[0:127, 3, :], in_=pTf[0:127, 3, :]), reads=['pTf'], writes=['pTb'], acc=True)
                pOc = pC if kv == 0 else pD
                pOcn = 'pC' if kv == 0 else 'pD'
                for ci_, (ch, nch) in enumerate(NCH):
                    S.op('pe', lambda e, ch=ch, nch=nch, ci_=ci_, pOc=pOc: e.matmul(pOc[0:4, 256:320], lhsT=pTb[0:nch, ch, :], rhs=Vcs[0:nch, ch, :], start=(ci_ == 0), stop=(ci_ == 3)),
                         reads=['pTb', 'Vcs'], writes=[pOcn], acc=True)
                for ci_, (ch, nch) in enumerate(NCH):
                    S.op('pe', lambda e, ch=ch, nch=nch, ci_=ci_: e.matmul(pm[0:4, 128:257], lhsT=pTf[0:nch, ch, :], rhs=cover_s[0:nch, ch, :], start=(ci_ == 0), stop=(ci_ == 3)),
                         reads=['pTf', 'cs'], writes=['pm'], acc=True)
                S.op('dve', lambda e: e.tensor_copy(out=s4[:, 0:129], in_=pm[0:4, 128:257]), reads=['pm'], writes=['s4'])
                S.op('pe', lambda e: e.matmul(pm[0:1, 300:429], lhsT=onesf[0:4, 0:1], rhs=s4[:, 0:129], start=True, stop=True), reads=['onesf', 's4'], writes=['pm'])
                S.op('dve', lambda e: e.tensor_copy(out=sco1[:, 0:129], in_=pm[0:1, 300:429]), reads=['pm'], writes=['sco1'])
                S.op('dve', lambda e: e.memset(sco1[:, 0:1], BIG), reads=['sco1'], writes=['sco1'])
                S.op('dve', lambda e: e.memset(sco1[:, 127:129], BIG), reads=['sco1'], writes=['sco1'])
                S.op('dve', lambda e: e.max(out=m81[:, 0:8], in_=sco1[:, 0:129]), reads=['sco1'], writes=['m81'])
                S.op('dve', lambda e: e.match_replace(out=wk1[:, 0:129], in_to_replace=m81[:, 0:8], in_values=sco1[:, 0:129], imm_value=-3.0e30), reads=['sco1', 'm81'], writes=['wk1'])
                S.op('dve', lambda e: e.max(out=m81[:, 8:16], in_=wk1[:, 0:129]), reads=['wk1', 'm81'], writes=['m81'])
                S.op('dve', lambda e: e.tensor_scalar(out=sco1[:, 0:129], in0=sco1[:, 0:129], scalar1=m81[:, 15:16], scalar2=-NEG, op0=ALU.is_ge, op1=ALU.mult), reads=['sco1', 'm81'], writes=['sco1'])
                S.op('dve', lambda e: e.tensor_scalar_add(out=sco1[:, 0:129], in0=sco1[:, 0:129], scalar1=NEG), reads=['sco1'], writes=['sco1'])
                S.op('dve', lambda e: e.tensor_copy(out=ndup[:], in_=sco1[:, 0:128].unsqueeze(2).to_broadcast([1, 128, 2])), reads=['sco1'], writes=['ndup'])
                ndf = ndup[:].rearrange("p b t -> p (b t)")
                for hs in range(2):
                    S.op('pe', lambda e, hs=hs, kv=kv: e.matmul(pm[:, 440 + kv * 2 + hs:441 + kv * 2 + hs], lhsT=ndf[0:1, hs * 128:(hs + 1) * 128], rhs=onesf[0:1, 0:1], start=True, stop=True),
                         reads=['ndup', 'onesf'], writes=['pm'], acc=True)
            S.op('dve', lambda e: e.tensor_copy(out=mcol[:], in_=pm[:, 440:444]), reads=['pm'], writes=['mcol'])
            store_o(0, s, [(pC[:, 256:320], 'pC', 64), (pD[:, 256:320], 'pD', 64)])
            for hs in range(2):
                gather(cache_slc, s, hs)
                for kv in range(2):
                    pX = pacc[kv]
                    pXn = 'pacc%d' % kv
                    for rg in range(4):
                        ct = cTr[tcnt[0] % 2]
                        ctn = 'cTr%d' % (tcnt[0] % 2)
                        tcnt[0] += 1
                        for rr in range(8):
                            r = rg * 8 + rr
                            S.op('pe', lambda e, rr=rr, r=r, hs=hs, kv=kv: e.transpose(out=pT[0:64, rr * 128:(rr + 1) * 128], in_=rawbf[:, hs, r * 256 + kv * 128:r * 256 + kv * 128 + 64], identity=ident[:]),
                                 reads=['rawbf%d' % hs, 'ident'], writes=['pT'], acc=(rr > 0))
                        S.op('dve', lambda e, ct=ct: e.tensor_copy(out=ct[0:64, :, :], in_=pT[0:64, :].rearrange("p (a t) -> p a t", a=8)), reads=['pT'], writes=[ctn])
                        for rr in range(8):
                            r = rg * 8 + rr
                            S.op('pe', lambda e, pX=pX, ct=ct, rr=rr, r=r, s=s, kv=kv: e.matmul(pX[:, r * 4:(r + 1) * 4], lhsT=ct[0:64, rr, :], rhs=qsT[:, s, kv * 4:(kv + 1) * 4], start=True, stop=True),
                                 reads=[ctn, 'qsT'], writes=[pXn], acc=True)
                    pxv = pX[:, 0:128].rearrange("p (r g) -> p r g", g=4)
                    if hs == 1:
                        S.op('dve', lambda e, pxv=pxv, kv=kv: e.tensor_tensor(out=ssb[:], in0=pxv, in1=Bn[:, :, kv * 4:(kv + 1) * 4], op=ALU.add), reads=[pXn, 'Bn'], writes=['ssb'])
                        S.op('act', lambda e, kv=kv, hs=hs: e.activation(out=PTs[:], in_=ssb[:], func=AF.Exp, bias=mcol[:, kv * 2 + hs:kv * 2 + hs + 1], scale=1.0), reads=['ssb', 'mcol'], writes=['PTs'])
                    else:
                        S.op('act', lambda e, pxv=pxv, kv=kv, hs=hs: e.activation(out=PTs[:], in_=pxv, func=AF.Exp, bias=mcol[:, kv * 2 + hs:kv * 2 + hs + 1], scale=1.0), reads=[pXn, 'mcol'], writes=['PTs'])
                    S.op('dve', lambda e, hs=hs, kv=kv: e.tensor_reduce(out=zp[:, hs, kv, :], in_=PTs[:].rearrange("p r g -> p g r"), axis=AX.X, op=ALU.add), reads=['PTs'], writes=['zp'], acc=True)
                    pO = pC if kv == 0 else pD
                    pOn = 'pC' if kv == 0 else 'pD'
                    for r in range(32):
                        S.op('pe', lambda e, pO=pO, r=r, hs=hs, kv=kv: e.matmul(pO[0:4, 0:64], lhsT=PTs[:, r, :], rhs=rawbf[:, hs, r * 256 + kv * 128 + 64:r * 256 + kv * 128 + 128],
                                                                                start=(hs == 0 and r == 0), stop=False),
                             reads=['PTs', 'rawbf%d' % hs], writes=[pOn], acc=True)
            for kv in range(2):
                pO = pC if kv == 0 else pD
                pOn = 'pC' if kv == 0 else 'pD'
                S.op('pe', lambda e, pO=pO, s=s, kv=kv: e.matmul(pO[0:4, 0:64], lhsT=pnm[1][:, s, kv * 4:(kv + 1) * 4], rhs=vnew[1][:, kv * 64:(kv + 1) * 64], start=False, stop=True),
                     reads=['pnm1', 'vnew1'], writes=[pOn], acc=True)
                for hs in range(2):
                    S.op('pe', lambda e, pO=pO, hs=hs, kv=kv: e.matmul(pO[0:4, 64:65], lhsT=zp[:, hs, kv, :], rhs=onesf[:, 0:1], start=(hs == 0), stop=False), reads=['zp', 'onesf'], writes=[pOn], acc=True)
                S.op('pe', lambda e, pO=pO, s=s, kv=kv: e.matmul(pO[0:4, 64:65], lhsT=pnm[1][:, s, kv * 4:(kv + 1) * 4], rhs=onesb[0:SB, 0:1], start=False, stop=True), reads=['pnm1', 'onesb'], writes=[pOn], acc=True)
            store_o(1, s, [(pC, 'pC', 65), (pD, 'pD', 65)])
            S.dma('sp', lambda e, s=s: e.dma_start(out=raww, in_=win_c[s].rearrange("(j r) f -> j r f", r=4)), 'raw0', writes=['raw0'], acc=False)
            S.op('act', lambda e: e.copy(out=rawwbf, in_=raww), reads=['raw0'], writes=['rawbf0'])
            for kv in range(2):
                for r in range(4):
                    S.op('pe', lambda e, r=r, kv=kv: e.transpose(out=pT[0:64, (kv * 4 + r) * 128:(kv * 4 + r + 1) * 128], in_=rawwbf[:, r, kv * 128:kv * 128 + 64], identity=ident[:]),
                         reads=['rawbf0', 'ident'], writes=['pT'], acc=(kv > 0 or r > 0))
            ct = cTr[tcnt[0] % 2]
            ctn = 'cTr%d' % (tcnt[0] % 2)
            tcnt[0] += 1
            S.op('dve', lambda e, ct=ct: e.tensor_copy(out=ct[0:64, :, :], in_=pT[0:64, :].rearrange("p (a t) -> p a t", a=8)), reads=['pT'], writes=[ctn])
            for kv in range(2):
                pX = pacc[kv]
                pXn = 'pacc%d' % kv
                for r in range(4):
                    S.op('pe', lambda e, pX=pX, ct=ct, r=r, s=s, kv=kv: e.matmul(pX[:, r * 4:(r + 1) * 4], lhsT=ct[0:64, kv * 4 + r, :], rhs=qsT[:, s, kv * 4:(kv + 1) * 4], start=True, stop=True),
                         reads=[ctn, 'qsT'], writes=[pXn], acc=True)
                S.op('dve', lambda e, pX=pX, kv=kv: e.tensor_tensor(out=ssb[:, 0:4, :], in0=pX[:, 0:16].rearrange("p (r g) -> p r g", g=4), in1=Bw[:, :, kv * 4:(kv + 1) * 4], op=ALU.add), reads=[pXn, 'Bw'], writes=['ssb'])
                S.op('act', lambda e: e.activation(out=PTw[:], in_=ssb[:, 0:4, :], func=AF.Exp), reads=['ssb'], writes=['PTw'])
                S.op('dve', lambda e, kv=kv: e.tensor_reduce(out=zp[:, 0, kv, :], in_=PTw[:].rearrange("p r g -> p g r"), axis=AX.X, op=ALU.add), reads=['PTw'], writes=['zp'], acc=True)
                pO = pC if kv == 0 else pD
                pOn = 'pC' if kv == 0 else 'pD'
                for r in range(4):
                    S.op('pe', lambda e, pO=pO, r=r, kv=kv: e.matmul(pO[0:4, 0:64], lhsT=PTw[:, r, :], rhs=rawwbf[:, r, kv * 128 + 64:kv * 128 + 128], start=(r == 0), stop=False),
                         reads=['PTw', 'rawbf0'], writes=[pOn], acc=True)
                S.op('pe', lambda e, pO=pO, s=s, kv=kv: e.matmul(pO[0:4, 0:64], lhsT=pnm[2][:, s, kv * 4:(kv + 1) * 4], rhs=vnew[2][:, kv * 64:(kv + 1) * 64], start=False, stop=True),
                     reads=['pnm2', 'vnew2'], writes=[pOn], acc=True)
                S.op('pe', lambda e, pO=pO, kv=kv: e.matmul(pO[0:4, 64:65], lhsT=zp[:, 0, kv, :], rhs=onesf[:, 0:1], start=True, stop=False), reads=['zp', 'onesf'], writes=[pOn], acc=True)
                S.op('pe', lambda e, pO=pO, s=s, kv=kv: e.matmul(pO[0:4, 64:65], lhsT=pnm[2][:, s, kv * 4:(kv + 1) * 4], rhs=onesb[0:SB, 0:1], start=False, stop=True), reads=['pnm2', 'onesb'], writes=[pOn], acc=True)
            store_o(2, s, [(pC, 'pC', 65), (pD, 'pD', 65)])
        S.barrier()
        phn.close()
        tok = sb("tok", [SB, 3, 8, 65], st=ph)
        S.dma('sp', lambda e: e.dma_start(out=tok[:], in_=o_scr.rearrange("b s k g e -> s b (k g) e")), 'tok', reads=['o_scr'], writes=['tok'])
        rzt = sb("rzt", [SB, 24], st=ph)
        S.op('dve', lambda e: e.reciprocal(out=rzt[:].rearrange("p (b h) -> p b h", b=3), in_=tok[:, :, :, 64]), reads=['tok'], writes=['rzt'])
        gsg = sb("gsg", [SB, 24], st=ph)
        S.op('act', lambda e: e.activation(out=gsg[:], in_=proj_s[:, O_GA:O_GA + 24], func=AF.Sigmoid), reads=['proj_s'], writes=['gsg'])
        S.op('dve', lambda e: e.tensor_tensor(out=gsg[:], in0=gsg[:], in1=rzt[:], op=ALU.mult), reads=['gsg', 'rzt'], writes=['gsg'])
        tokv = tok[:].rearrange("p b h e -> p (b h) e")
        S.op('dve', lambda e: e.tensor_tensor(out=tokv[:, :, 0:64], in0=tokv[:, :, 0:64], in1=gsg[:].unsqueeze(2).to_broadcast([SB, 24, 64]), op=ALU.mult), reads=['tok', 'gsg'], writes=['tok'])
        oas = sb("oas", [SB, 8, 64], st=ph)
        S.op('dve', lambda e: e.tensor_tensor(out=oas[:], in0=tok[:, 0, :, 0:64], in1=tok[:, 1, :, 0:64], op=ALU.add), reads=['tok'], writes=['oas'])
        S.op('dve', lambda e: e.tensor_tensor(out=oas[:], in0=oas[:], in1=tok[:, 2, :, 0:64], op=ALU.add), reads=['tok', 'oas'], writes=['oas'])
        zsg = sb("zsg", [SB, 512], st=ph)
        S.op('act', lambda e: e.activation(out=zsg[:], in_=proj_s[:, O_ZA:O_ZA + 512], func=AF.Sigmoid), reads=['proj_s'], writes=['zsg'])
        S.op('dve', lambda e: e.tensor_tensor(out=zsg[:], in0=zsg[:], in1=proj_s[:, O_ZA:O_ZA + 512], op=ALU.mult), reads=['zsg', 'proj_s'], writes=['zsg'])
        S.op('dve', lambda e: e.tensor_tensor(out=mix_s[:, 512:1024], in0=oas[:].rearrange("p h e -> p (h e)"), in1=zsg[:], op=ALU.mult), reads=['oas', 'zsg'], writes=['mix_s'], acc=True)
        mixs_bf = sb("mixs_bf", [SB, 1024], BF16, st=ph)
        S.op('dve', lambda e: e.tensor_copy(out=mixs_bf[:], in_=mix_s[:]), reads=['mix_s'], writes=['mixs_bf'])
        mixTs = sb("mixTs", [128, 8, SB], BF16, st=ph)
        for k in range(8):
            S.op('pe', lambda e, k=k: e.transpose(out=pT[:, k * 16:(k + 1) * 16], in_=mixs_bf[:, k * 128:(k + 1) * 128], identity=ident[0:SB, 0:SB]),
                 reads=['mixs_bf', 'ident'], writes=['pT'], acc=(k > 0))
        S.op('dve', lambda e: e.tensor_copy(out=mixTs[:], in_=pT[:, 0:128].rearrange("p (k s) -> p k s", k=8)), reads=['pT'], writes=['mixTs'])
        zy = sb("zy", [SB, D], st=ph)
        for c in range(2):
            for k in range(8):
                S.op('pe', lambda e, c=c, k=k: e.matmul(pacc[c][0:SB, :], lhsT=mixTs[:, k, :], rhs=w_out_bf[:, k, c * 512:(c + 1) * 512], start=(k == 0), stop=(k == 7)),
                     reads=['mixTs', 'w_out_bf'], writes=['pacc%d' % c], acc=(k > 0))
            S.op('dve', lambda e, c=c: e.tensor_tensor(out=zy[:, c * 512:(c + 1) * 512], in0=pacc[c][0:SB, :], in1=bout_bc[0:SB, c * 512:(c + 1) * 512], op=ALU.add),
                 reads=['pacc%d' % c, 'bout_bc'], writes=['zy'], acc=(c > 0))
        S.op('dve', lambda e: e.tensor_tensor(out=zy[:], in0=zy[:], in1=mod_t[0:SB, 2 * D:3 * D], op=ALU.mult), reads=['zy', 'mod_t'], writes=['zy'])
        S.op('dve', lambda e: e.scalar_tensor_tensor(out=zy[:], in0=xs[:], scalar=ALPHA, in1=zy[:], op0=ALU.mult, op1=ALU.add), reads=['xs', 'zy'], writes=['zy'])
        layer_norm_stats(zy, 'zy', SB, mvs, st6s, 's')
        S.op('dve', lambda e: e.tensor_scalar(out=zy[:], in0=zy[:], scalar1=mvs[0:SB, 0:1], scalar2=mvs[0:SB, 2:3], op0=ALU.subtract, op1=ALU.mult), reads=['zy', 'mvs'], writes=['zy'])
        S.op('dve', lambda e: e.tensor_tensor(out=zy[:], in0=zy[:], in1=lng_bc[0:SB, :], op=ALU.mult), reads=['zy', 'lng_bc'], writes=['zy'])
        S.op('dve', lambda e: e.tensor_tensor(out=zy[:], in0=zy[:], in1=lnb_bc[0:SB, :], op=ALU.add), reads=['zy', 'lnb_bc'], writes=['zy'])
        S.dma('sp', lambda e: e.dma_start(out=o_y_s, in_=zy[:]), 'zy', reads=['zy'])
        S.barrier()

    mid.close()

    with ExitStack() as ph:
        xt = [sb("xt0", [128, D], st=ph), sb("xt1", [128, D], st=ph)]
        mvp = sb("mvp", [128, 4], st=ph)
        st6p = sb("st6p", [128, 2, 6], st=ph)
        xn = sb("xn", [128, D], BF16, st=ph)
        hT = sb("hT", [128, 8, 128], BF16, st=ph)
        proj = sb("proj", [128, DIN], st=ph)
        kvp = sb("kvp", [128, 768], st=ph)
        gt = sb("gt", [128, 32], st=ph)
        NG = sb("NG", [128, 4], st=ph)
        runmax = sb("runmax", [128, 4], st=ph)
        qk_bf = sb("qk_bf", [128, 1024], BF16, st=ph)
        qkT = sb("qkT", [128, 8, 128], BF16, st=ph)
        SmT = sb("SmT", [128, 128], BF16, st=ph)
        vpe = sb("vpe", [128, 4, 129], BF16, st=ph)
        ve1 = sb("ve1", [128, 4, 129], BF16, st=ph)
        kp = sb("kp", [128, 4, 128], BF16, st=ph)
        Cst = sb("Cst", [128, 4, 129], st=ph)
        Cbf = sb("Cbf", [128, 4, 129], BF16, st=ph)
        hraw = sb("hraw", [128, 4, 128], st=ph)
        dn = sb("dn", [128, 8], st=ph)
        st6h = sb("st6h", [128, 4, 6], st=ph)
        mvh = sb("mvh", [128, 4, 2], st=ph)
        rsh = sb("rsh", [128, 4], st=ph)
        sig = sb("sig", [128, 1024], st=ph)
        mix = sb("mix", [128, 1024], BF16, st=ph)

        rb_bc = sb("rb_bc", [128, 32, 8], st=ph)
        S.dma('sp', lambda e: e.dma_start(out=rb_bc[:].rearrange("p b h -> p (b h)"), in_=rel_bias.partition_broadcast(128)), 'rb_bc', writes=['rb_bc'])
        rbd = sb("rbd", [128, 32, 8], st=ph)
        S.op('dve', lambda e: e.tensor_tensor(out=rbd[:], in0=rb_bc[:], in1=rb_bc[:, 31:32, :].to_broadcast([128, 32, 8]), op=ALU.subtract), reads=['rb_bc'], writes=['rbd'])
        DT = sb("DT", [128, 2, 8, 128], BF16, st=ph)
        wm4r = sb("wm4r", [128, 4, 128], BF16, st=ph)
        E_ext = sb("E_ext", [128, 8, 510], BF16, st=ph)
        amdt = sb("amdt", [128, 3, 128], st=ph)
        S.dma('sp', lambda e: e.dma_start(out=amdt[:].rearrange("p a t -> p (a t)"), in_=am_dt), 'amdt', writes=['amdt'])
        S.op('dve', lambda e: e.tensor_copy(out=wm4r[:], in_=amdt[:, 2:3, :].to_broadcast([128, 4, 128])), reads=['amdt'], writes=['wm4r'])
        W1bf = sb("W1bf", [64, 2, 32, 64], BF16, st=ph)
        Eblk = sb("Eblk", [128, T], BF16, st=ph)
        cbias = sb("cbias", [64, 2], st=ph)
        w2p = sb("w2p", [64, 2, 128], BF16, st=ph)
        with ExitStack() as ph2:
            oh = sb("oh", [128, 32, 128], st=ph2)
            tmpo = sb("tmpo", [128, 32, 128], st=ph2)
            dtf = sb("dtf", [128, 128], st=ph2)
            for v in range(2):
                S.dma('sp', lambda e, v=v: e.dma_start(out=oh[:].rearrange("p b t -> p (b t)"), in_=oh_dt[v]), 'oh', writes=['oh'], acc=False)
                for h in range(8):
                    S.op('dve', lambda e, h=h: e.tensor_tensor(out=tmpo[:], in0=oh[:], in1=rbd[:, :, h:h + 1].to_broadcast([128, 32, 128]), op=ALU.mult),
                         reads=['oh', 'rbd'], writes=['tmpo'])
                    S.op('dve', lambda e: e.tensor_reduce(out=dtf[:], in_=tmpo[:].rearrange("p b t -> p t b"), axis=AX.X, op=ALU.add), reads=['tmpo'], writes=['dtf'])
                    S.op('dve', lambda e, v=v, h=h: e.tensor_tensor(out=DT[:, v, h, :], in0=dtf[:], in1=amdt[:, 1 - v, :], op=ALU.add), reads=['dtf', 'amdt'], writes=['DT'], acc=True)
            for hf in range(2):
                S.dma('sp', lambda e, hf=hf: e.dma_start(out=tmpo[:].rearrange("p b t -> p (b t)")[:, 0:2048], in_=eblk_d[:, hf * 2048:(hf + 1) * 2048]), 'tmpo_d', writes=['tmpo'], acc=False)
                S.op('act', lambda e, hf=hf: e.copy(out=Eblk[:, hf * 2048:(hf + 1) * 2048], in_=tmpo[:].rearrange("p b t -> p (b t)")[:, 0:2048]), reads=['tmpo'], writes=['Eblk'], acc=True)
            ohc = sb("ohc", [128, 32, 16], st=ph2)
            amc = sb("amc", [128, 16], st=ph2)
            S.dma('sp', lambda e: e.dma_start(out=ohc[:].rearrange("p b m -> p (b m)"), in_=oh_c), 'ohc', writes=['ohc'])
            S.dma('sp', lambda e: e.dma_start(out=amc[:], in_=am_c), 'amc', writes=['amc'])
            S.op('pool', lambda e: e.memset(E_ext[:], NEG), writes=['E_ext'])
            for h in range(8):
                S.op('dve', lambda e, h=h: e.tensor_tensor(out=tmpo[:, :, 0:16], in0=ohc[:], in1=rb_bc[:, :, h:h + 1].to_broadcast([128, 32, 16]), op=ALU.mult),
                     reads=['ohc', 'rb_bc'], writes=['tmpo'])
                S.op('dve', lambda e: e.tensor_reduce(out=dtf[:, 0:16], in_=tmpo[:, :, 0:16].rearrange("p b t -> p t b"), axis=AX.X, op=ALU.add), reads=['tmpo'], writes=['dtf'])
                S.op('dve', lambda e, h=h: e.tensor_tensor(out=E_ext[:, h, 246:262], in0=dtf[:, 0:16], in1=amc[:], op=ALU.add), reads=['dtf', 'amc', 'E_ext'], writes=['E_ext'], acc=True)
                S.op('dve', lambda e, h=h: e.tensor_copy(out=E_ext[:, h, 0:246], in_=rb_bc[:, 31, h:h + 1].to_broadcast([128, 246])), reads=['rb_bc', 'E_ext'], writes=['E_ext'], acc=True)
            load_cmp_weights(ph2, W1bf, cbias, w2p)
            for k in range(8):
                S.op('dve', lambda e, k=k: e.tensor_tensor(out=w_out_bf[:, k, :], in0=w_out_bf[:, k, :], in1=gate_bc[:], op=ALU.mult), reads=['w_out_bf', 'gate_bc'], writes=['w_out_bf'])
            S.op('dve', lambda e: e.tensor_tensor(out=bout_bc[:], in0=bout_bc[:], in1=gate_bc[:], op=ALU.mult), reads=['bout_bc', 'gate_bc'], writes=['bout_bc'])
            S.barrier()
        bigc = sb("bigc", [128, 1], st=ph)
        S.op('dve', lambda e: e.memset(bigc[:], BIG), writes=['bigc'])
        tiny = 1e-30
        skT = sb("skT", [128, T], BF16, st=ph)
        svx = sb("svx", [128, NT, 2, 65], BF16, st=ph)
        wkT = sb("wkT", [128, 8, 128], BF16, st=ph)
        wvx = sb("wvx", [128, 8, 2, 65], BF16, st=ph)
        S.op('pool', lambda e: e.memset(svx[:], 1.0), writes=['svx'])
        S.op('pool', lambda e: e.memset(wvx[:], 1.0), writes=['wvx'])
        KcT = sb("KcT", [128, 256], BF16, st=ph)
        VcT = sb("VcT", [128, 256], BF16, st=ph)
        S.op('pool', lambda e: e.memset(KcT[:], 0.0), writes=['KcT'])
        S.op('pool', lambda e: e.memset(VcT[:], 0.0), writes=['VcT'])
        Vc_sb = sb("Vc_sb", [128, 2, 128], BF16, st=ph)
        nsa_bf = sb("nsa_bf", [128, 1024], BF16, st=ph)
        qaT = sb("qaT", [128, 4, 128], BF16, st=ph)
        qaTz = [sb("qaTz0", [128, 4, 128], BF16, st=ph), sb("qaTz1", [128, 4, 128], BF16, st=ph)]
        S.op('pool', lambda e: e.memset(qaTz[0][:], 0.0), writes=['qaTz0'])
        S.op('pool', lambda e: e.memset(qaTz[1][:], 0.0), writes=['qaTz1'])
        cT = sb("cT", [64, 4, 128], BF16, st=ph)
        Asb = sb("Asb", [64, 2, 2, 2, 8], st=ph)
        carry = sb("carry", [64, 2, 2], st=ph)
        S.op('dve', lambda e: e.memset(carry[:], 0.0), writes=['carry'])
        pre = sb("pre", [64, 2, 2, 8], st=ph)
        gx = sb("gx", [64, 2, 2, 8], st=ph)
        gbf = sb("gbf", [64, 2, 2, 8], BF16, st=ph)
        sc = sb("sc", [128, 4, 256], st=ph)
        pbf = sb("pbf", [128, 4, 256], BF16, st=ph)
        pcT = sb("pcT", [128, 8, 128], BF16, st=ph)
        zc = sb("zc", [128, 8], st=ph)
        imp = sb("imp", [128, 256], st=ph)
        impT = sb("impT", [128, 2, 128], st=ph)
        sco = sb("sco", [128, 64], st=ph)
        wk_ = sb("wk_", [128, 64], st=ph)
        m8 = sb("m8", [128, 16], st=ph)
        negm = sb("negm", [128, 2, 64], st=ph)
        negmZ = [sb("negmZ0", [128, 4, 128], BF16, st=ph), sb("negmZ1", [128, 4, 128], BF16, st=ph)]
        S.op('pool', lambda e: e.memset(negmZ[0][:], 0.0), writes=['negmZ0'])
        S.op('pool', lambda e: e.memset(negmZ[1][:], 0.0), writes=['negmZ1'])
        PTb = [sb("PTb0", [128, 512], BF16, st=ph), sb("PTb1", [128, 512], BF16, st=ph)]
        OTs = sb("OTs", [65, 512], st=ph)
        oc = sb("oc", [128, 3, 8, 64], st=ph)
        gsig = sb("gsig", [128, 24], st=ph)
        oall = sb("oall", [128, 512], st=ph)
        mixT = sb("mixT", [128, 8, 128], BF16, st=ph)
        S.op('dve', lambda e: e.memset(NG[:], 0.0), writes=['NG'])
        S.op('dve', lambda e: e.memset(runmax[:], 0.0), writes=['runmax'])
        S.op('dve', lambda e: e.memset(Cst[:], 0.0), writes=['Cst'])
        S.op('pool', lambda e: e.memset(ve1[:], 1.0), writes=['ve1'])
        def load_x(ti):
            xb = xt[ti % 2]
            xnm = 'xt%d' % (ti % 2)
            S.dma('sp', lambda e, xb=xb, ti=ti: e.dma_start(out=xb[:], in_=x_p[ti * 128:(ti + 1) * 128, :]), xnm, writes=[xnm], acc=False)
        load_x(0)
        for ti in range(NT_RUN):
            xb = xt[ti % 2]
            xnm = 'xt%d' % (ti % 2)
            if ti + 1 < NT_RUN:
                load_x(ti + 1)
            layer_norm_stats(xb, xnm, 128, mvp, st6p, 'p')
            S.op('dve', lambda e, xb=xb: e.tensor_scalar(out=xn[:], in0=xb[:], scalar1=mvp[:, 0:1], scalar2=mvp[:, 2:3], op0=ALU.subtract, op1=ALU.mult),
                 reads=[xnm, 'mvp'], writes=['xn'])
            for k in range(8):
                S.op('pe', lambda e, k=k: e.transpose(out=pT[:, k * 128:(k + 1) * 128], in_=xn[:, k * 128:(k + 1) * 128], identity=ident[:]),
                     reads=['xn', 'ident'], writes=['pT'], acc=(k > 0))
            for k in range(8):
                S.op('act', lambda e, k=k: e.activation(out=hT[:, k, :], in_=pT[:, k * 128:(k + 1) * 128], func=AF.Identity,
                                                        bias=modT[:, k:k + 1], scale=modT[:, 8 + k:9 + k]),
                     reads=['pT', 'modT'], writes=['hT'], acc=(k > 0))
            in_proj(128, lambda k: hT[:, k, :], 'hT', proj, 'proj', 'act')

            KS = 128.0 ** -0.5
            S.op('act', lambda e: e.activation(out=gt[:, 0:4], in_=proj[:, O_FM:O_FM + 4], func=AF.Exp, scale=-1.0), reads=['proj'], writes=['gt'])
            S.op('act', lambda e: e.activation(out=gt[:, 0:4], in_=gt[:, 0:4], func=AF.Ln, bias=onesf[:, 0:1], scale=1.0), reads=['gt', 'onesf'], writes=['gt'])
            S.op('pe', lambda e: e.matmul(pm[:, 0:4], lhsT=tri_f, rhs=gt[:, 0:4], start=True, stop=True), reads=['gt', 'ident_f'], writes=['pm'])
            S.op('pe', lambda e: e.matmul(pm[:, 4:8], lhsT=onesf[:], rhs=gt[:, 0:4], start=True, stop=True), reads=['gt', 'onesf'], writes=['pm'], acc=True)
            S.op('dve', lambda e: e.tensor_tensor(out=gt[:, 4:8], in0=pm[:, 0:4], in1=proj[:, O_IM:O_IM + 4], op=ALU.add), reads=['pm', 'proj'], writes=['gt'])
            S.op('act', lambda e: e.activation(out=gt[:, 8:12], in_=gt[:, 4:8], func=AF.Exp), reads=['gt'], writes=['gt'])
            S.op('act', lambda e: e.activation(out=gt[:, 12:20], in_=pm[:, 0:8], func=AF.Exp, scale=-1.0), reads=['pm'], writes=['gt'])
            S.op('dve', lambda e: e.tensor_tensor(out=gt[:, 20:24], in0=gt[:, 4:8], in1=NG[:], op=ALU.add), reads=['gt', 'NG'], writes=['gt'])
            S.op('dve', lambda e: e.tensor_tensor(out=runmax[:], in0=runmax[:], in1=gt[:, 20:24], op=ALU.max), reads=['gt', 'runmax'], writes=['runmax'])
            S.op('dve', lambda e: e.tensor_tensor(out=NG[:], in0=NG[:], in1=pm[:, 4:8], op=ALU.add), reads=['pm', 'NG'], writes=['NG'])
            ea = gt[:, 8:12]
            eb = gt[:, 12:16]
            ebe = gt[:, 16:20]
            S.op('act', lambda e: e.copy(out=qk_bf[:, 0:512], in_=proj[:, O_QM:O_QM + 512]), reads=['proj'], writes=['qk_bf'])
            S.op('act', lambda e: e.mul(out=qk_bf[:, 512:1024], in_=proj[:, O_KM:O_KM + 512], mul=KS), reads=['proj'], writes=['qk_bf'], acc=True)
            for k in range(8):
                S.op('pe', lambda e, k=k: e.transpose(out=pT[:, k * 128:(k + 1) * 128], in_=qk_bf[:, k * 128:(k + 1) * 128], identity=ident[:]),
                     reads=['qk_bf', 'ident'], writes=['pT'], acc=(k > 0))
            S.op('dve', lambda e: e.tensor_copy(out=qkT[:], in_=pT[:].rearrange("p (k t) -> p k t", k=8)), reads=['pT'], writes=['qkT'])
            vv = proj[:, O_VM:O_VM + 512].rearrange("p (h d) -> p h d", h=4)
            S.op('pool', lambda e: e.tensor_copy(out=ve1[:, :, 0:128], in_=vv), reads=['proj'], writes=['ve1'])
            S.op('dve', lambda e: e.tensor_tensor(out=vpe[:, :, 0:128], in0=vv, in1=ea.unsqueeze(2).to_broadcast([128, 4, 128]), op=ALU.mult),
                 reads=['proj', 'gt'], writes=['vpe'])
            S.op('dve', lambda e: e.tensor_copy(out=vpe[:, :, 128:129], in_=ea.unsqueeze(2)), reads=['gt'], writes=['vpe'], acc=True)
            S.op('dve', lambda e: e.tensor_tensor(out=kp[:], in0=qk_bf[:, 512:1024].rearrange("p (h d) -> p h d", h=4),
                                                  in1=ea.unsqueeze(2).to_broadcast([128, 4, 128]), op=ALU.mult), reads=['qk_bf', 'gt'], writes=['kp'])
            for h in range(4):
                S.op('pe', lambda e, h=h: e.matmul(pS[:, 0:128], lhsT=qkT[:, 4 + h, :], rhs=qkT[:, h, :], start=True, stop=True), reads=['qkT'], writes=['pS'])
                S.op('dve', lambda e: e.tensor_tensor(out=SmT[:], in0=pS[:, 0:128], in1=tri_f, op=ALU.mult), reads=['pS', 'ident_f'], writes=['SmT'])
                S.op('pe', lambda e, h=h: e.matmul(pN[:, 0:129], lhsT=SmT[:], rhs=vpe[:, h, :], start=True, stop=(ti == 0)), reads=['SmT', 'vpe'], writes=['pN'])
                if ti > 0:
                    S.op('pe', lambda e, h=h: e.matmul(pN[:, 0:129], lhsT=qkT[:, h, :], rhs=Cbf[:, h, :], start=False, stop=True), reads=['qkT', 'Cbf'], writes=['pN'], acc=True)
                S.op('act', lambda e, h=h: e.activation(out=dn[:, 0:1], in_=pN[:, 128:129], func=AF.Abs, scale=eb[:, h:h + 1]),
                     reads=['pN', 'gt'], writes=['dn'])
                S.op('dve', lambda e: e.tensor_scalar_max(out=dn[:, 0:1], in0=dn[:, 0:1], scalar1=1.0), reads=['dn'], writes=['dn'])
                S.op('dve', lambda e: e.reciprocal(out=dn[:, 0:1], in_=dn[:, 0:1]), reads=['dn'], writes=['dn'])
                S.op('dve', lambda e, h=h: e.tensor_tensor(out=dn[:, 1:2], in0=dn[:, 0:1], in1=eb[:, h:h + 1], op=ALU.mult), reads=['dn', 'gt'], writes=['dn'])
                S.op('act', lambda e, h=h: e.activation(out=hraw[:, h, :], in_=pN[:, 0:128], func=AF.Copy, scale=dn[:, 1:2]), reads=['pN', 'dn'], writes=['hraw'], acc=(h > 0))
                S.op('pe', lambda e, h=h: e.matmul(pC[:, 0:129], lhsT=kp[:, h, :], rhs=ve1[:, h, :], start=True, stop=True), reads=['kp', 've1'], writes=['pC'])
                S.op('dve', lambda e, h=h: e.tensor_scalar(out=Cst[:, h, :], in0=Cst[:, h, :], scalar1=ebe[:, h:h + 1], scalar2=None, op0=ALU.mult),
                     reads=['Cst', 'gt'], writes=['Cst'])
                S.op('dve', lambda e, h=h: e.scalar_tensor_tensor(out=Cst[:, h, :], in0=pC[:, 0:129], scalar=ebe[:, h:h + 1], in1=Cst[:, h, :], op0=ALU.mult, op1=ALU.add),
                     reads=['pC', 'Cst', 'gt'], writes=['Cst'])
            S.op('act', lambda e: e.copy(out=Cbf[:], in_=Cst[:]), reads=['Cst'], writes=['Cbf'])
            for h in range(4):
                S.op('dve', lambda e, h=h: e.bn_stats(out=st6h[:, h, :], in_=hraw[:, h, :]), reads=['hraw'], writes=['st6h'], acc=(h > 0))
            for h in range(4):
                S.op('dve', lambda e, h=h: e.bn_aggr(out=mvh[:, h, :], in_=st6h[:, h, :]), reads=['st6h'], writes=['mvh'], acc=(h > 0))
            S.op('act', lambda e: e.activation(out=rsh[:], in_=mvh[:, :, 1], func=AF.Sqrt, bias=epsc[:, :], scale=1.0), reads=['mvh', 'epsc'], writes=['rsh'])
            S.op('dve', lambda e: e.reciprocal(out=rsh[:], in_=rsh[:]), reads=['rsh'], writes=['rsh'])
            S.op('dve', lambda e: e.tensor_tensor(out=hraw[:], in0=hraw[:], in1=mvh[:, :, 0:1].to_broadcast([128, 4, 128]), op=ALU.subtract), reads=['hraw', 'mvh'], writes=['hraw'])
            S.op('dve', lambda e: e.tensor_tensor(out=hraw[:], in0=hraw[:], in1=rsh[:].unsqueeze(2).to_broadcast([128, 4, 128]), op=ALU.mult), reads=['hraw', 'rsh'], writes=['hraw'])
            hr = hraw[:].rearrange("p h d -> p (h d)")
            S.op('pool', lambda e: e.tensor_tensor(out=hr, in0=hr, in1=ng_bc[:], op=ALU.mult), reads=['hraw', 'ng_bc'], writes=['hraw'])
            S.op('act', lambda e: e.activation(out=sig[:], in_=proj[:, O_OM:O_OM + 1024], func=AF.Sigmoid), reads=['proj'], writes=['sig'])
            S.op('pool', lambda e: e.tensor_tensor(out=sig[:, 0:512], in0=sig[:, 0:512], in1=sig[:, 512:1024], op=ALU.mult), reads=['sig'], writes=['sig'])
            S.op('pool', lambda e: e.tensor_tensor(out=sig[:, 0:512], in0=sig[:, 0:512], in1=proj[:, O_ZM:O_ZM + 512], op=ALU.mult), reads=['sig', 'proj'], writes=['sig'])
            S.op('dve', lambda e: e.tensor_tensor(out=mix[:, 0:512], in0=hr, in1=sig[:, 0:512], op=ALU.mult), reads=['hraw', 'sig'], writes=['mix'])

            S.skip = DBG_ST < 1
            S.skip = (DBG_ST < 1) or bool(DBG_X & 1)
            for kv in range(2):
                S.op('act', lambda e, kv=kv: e.mul(out=nsa_bf[:, 0:512].rearrange("p (g k d) -> p k g d", g=4, k=2)[:, kv, :, :],
                                                  in_=proj[:, O_QA + kv * 256:O_QA + (kv + 1) * 256].rearrange("p (g d) -> p g d", g=4), mul=ATT_SCALE),
                     reads=['proj'], writes=['nsa_bf'], acc=(kv > 0))
            S.skip = (DBG_ST < 1) or bool(DBG_X & 2)
            S.op('pool', lambda e: e.tensor_copy(out=nsa_bf[:, 512:768], in_=proj[:, O_CK:O_CK + 256]), reads=['proj'], writes=['nsa_bf'], acc=True)
            S.op('pool', lambda e: e.tensor_copy(out=nsa_bf[:, 768:896], in_=proj[:, O_SK:O_SK + 128]), reads=['proj'], writes=['nsa_bf'], acc=True)
            S.op('pool', lambda e: e.tensor_copy(out=nsa_bf[:, 896:1024], in_=proj[:, O_WK:O_WK + 128]), reads=['proj'], writes=['nsa_bf'], acc=True)
            S.skip = (DBG_ST < 1) or bool(DBG_X & 4)
            S.op('dve', lambda e, ti=ti: e.tensor_copy(out=svx[:, ti, :, 0:64], in_=proj[:, O_SV:O_SV + 128].rearrange("p (k d) -> p k d", k=2)), reads=['proj'], writes=['svx'], acc=True)
            S.op('dve', lambda e, ti=ti: e.tensor_copy(out=wvx[:, ti % 8, :, 0:64], in_=proj[:, O_WV:O_WV + 128].rearrange("p (k d) -> p k d", k=2)), reads=['proj'], writes=['wvx'], acc=True)
            S.skip = (DBG_ST < 1) or bool(DBG_X & 8)
            for g in range(4):
                S.op('pe', lambda e, g=g: e.transpose(out=pT[:, g * 128:(g + 1) * 128], in_=nsa_bf[:, g * 128:(g + 1) * 128], identity=ident[:]),
                     reads=['nsa_bf', 'ident'], writes=['pT'], acc=(g > 0))
            S.op('pe', lambda e: e.transpose(out=pT[:, 512:640], in_=nsa_bf[:, 768:896], identity=ident[:]), reads=['nsa_bf', 'ident'], writes=['pT'], acc=True)
            S.op('pe', lambda e: e.transpose(out=pT[:, 640:768], in_=nsa_bf[:, 896:1024], identity=ident[:]), reads=['nsa_bf', 'ident'], writes=['pT'], acc=True)
            S.skip = (DBG_ST < 1) or bool(DBG_X & 16)
            S.op('dve', lambda e: e.tensor_copy(out=qaT[:], in_=pT[:, 0:512].rearrange("p (g t) -> p g t", g=4)), reads=['pT'], writes=['qaT'])
            S.skip = (DBG_ST < 1) or bool(DBG_X & 32)
            S.op('dve', lambda e, ti=ti: e.tensor_copy(out=skT[:, ti * 128:(ti + 1) * 128], in_=pT[:, 512:640]), reads=['pT'], writes=['skT'], acc=True)
            S.op('dve', lambda e, ti=ti: e.tensor_copy(out=wkT[:, ti % 8, :], in_=pT[:, 640:768]), reads=['pT'], writes=['wkT'], acc=True)
            S.skip = (DBG_ST < 1) or bool(DBG_X & 64)
            S.op('dve', lambda e: e.tensor_copy(out=qaTz[0][0:64, :, :], in_=qaT[0:64, :, :]), reads=['qaT'], writes=['qaTz0'])
            S.op('dve', lambda e: e.tensor_copy(out=qaTz[1][64:128, :, :], in_=qaT[64:128, :, :]), reads=['qaT'], writes=['qaTz1'])
            S.skip = DBG_ST < 2
            for j4 in range(4):
                S.op('pe', lambda e, j4=j4: e.transpose(out=pT[0:64, j4 * 128:(j4 + 1) * 128], in_=nsa_bf[:, 512 + j4 * 64:512 + (j4 + 1) * 64], identity=ident[:]),
                     reads=['nsa_bf', 'ident'], writes=['pT'], acc=(j4 > 0))
            S.op('dve', lambda e: e.tensor_copy(out=cT[:], in_=pT[0:64, 0:512].rearrange("p (a t) -> p a t", a=4)), reads=['pT'], writes=['cT'])
            cTv = cT[:].rearrange("p a (n q) -> p a n q", q=16)
            for c in range(2):
                for j in range(2):
                    slot = c * 2 + j
                    for pp in range(16):
                        S.op('pe', lambda e, c=c, j=j, slot=slot, pp=pp: e.matmul(pm[0:64, slot * 16:(slot + 1) * 16].rearrange("p (k n) -> p k n", k=2),
                                                                              lhsT=W1bf[:, c, j * 16 + pp, :], rhs=cTv[:, 2 * c:2 * c + 2, :, pp],
                                                                              start=(pp == 0), stop=(pp == 15)),
                             reads=['W1bf', 'cT'], writes=['pm'], acc=True)
            S.op('dve', lambda e: e.tensor_copy(out=Asb[:].rearrange("p c j k n -> p (c j k n)"), in_=pm[0:64, 0:64]), reads=['pm'], writes=['Asb'])
            S.op('dve', lambda e: e.tensor_tensor(out=pre[:, :, :, 1:8], in0=Asb[:, :, 0, :, 0:7], in1=Asb[:, :, 1, :, 1:8], op=ALU.add), reads=['Asb'], writes=['pre'])
            S.op('dve', lambda e: e.tensor_tensor(out=pre[:, :, :, 0], in0=carry[:], in1=Asb[:, :, 1, :, 0], op=ALU.add), reads=['Asb', 'carry'], writes=['pre'], acc=True)
            S.op('dve', lambda e: e.tensor_copy(out=carry[:], in_=Asb[:, :, 0, :, 7]), reads=['Asb', 'pre'], writes=['carry'])
            for c in range(2):
                S.op('act', lambda e, c=c: e.activation(out=gx[:, c, :, :], in_=pre[:, c, :, :], func=AF.Identity, bias=cbias[:, c:c + 1], scale=1.0),
                     reads=['pre', 'cbias'], writes=['gx'], acc=(c > 0))
            gxf = gx[:].rearrange("p c k n -> p (c k n)")
            pref = pre[:].rearrange("p c k n -> p (c k n)")
            S.op('dve', lambda e: e.tensor_tensor(out=pref, in0=gxf, in1=gxf, op=ALU.mult), reads=['gx', 'pre'], writes=['pre'])
            S.op('dve', lambda e: e.tensor_scalar(out=pref, in0=pref, scalar1=0.044715, scalar2=1.0, op0=ALU.mult, op1=ALU.add), reads=['pre'], writes=['pre'])
            S.op('dve', lambda e: e.tensor_tensor(out=pref, in0=pref, in1=gxf, op=ALU.mult), reads=['pre', 'gx'], writes=['pre'])
            S.op('act', lambda e: e.activation(out=pref, in_=pref, func=AF.Sigmoid, scale=1.5957691216), reads=['pre'], writes=['pre'])
            S.op('dve', lambda e: e.tensor_tensor(out=gbf[:].rearrange("p c k n -> p (c k n)"), in0=pref, in1=gxf, op=ALU.mult), reads=['pre', 'gx'], writes=['gbf'])
            i_lo = 1 if ti == 0 else 0
            n0 = 8 * ti - 1
            for c, dstT, dn_ in ((0, KcT, 'KcT'), (1, VcT, 'VcT')):
                S.op('pe', lambda e, c=c: e.matmul(pm[0:64, 64 + c * 16:72 + c * 16], lhsT=w2p[:, c, 64:128], rhs=gbf[:, c, 0, :], start=True, stop=True),
                     reads=['w2p', 'gbf'], writes=['pm'], acc=True)
                S.op('pe', lambda e, c=c: e.matmul(pm[:, 72 + c * 16:80 + c * 16], lhsT=w2p[:, c, :], rhs=gbf[:, c, 1, :], start=True, stop=True),
                     reads=['w2p', 'gbf'], writes=['pm'], acc=True)
                S.op('dve', lambda e, c=c, dstT=dstT: e.tensor_copy(out=dstT[0:64, n0 + i_lo:n0 + 8], in_=pm[0:64, 64 + c * 16 + i_lo:72 + c * 16]), reads=['pm'], writes=[dn_], acc=True)
                S.op('dve', lambda e, c=c, dstT=dstT: e.tensor_copy(out=dstT[64:128, n0 + i_lo:n0 + 8], in_=pm[64:128, 72 + c * 16 + i_lo:80 + c * 16]), reads=['pm'], writes=[dn_], acc=True)
            N = 8 * ti + 7
            S.skip = DBG_ST < 3
            nA = min(N, 128)
            nB = N - nA
            chunks = [(0, nA)] + ([(1, nB)] if nB > 0 else [])
            for c, nc_ in chunks:
                S.op('pe', lambda e, c=c, nc_=nc_: e.transpose(out=pT[0:nc_, c * 128:(c + 1) * 128], in_=VcT[:, c * 128:c * 128 + nc_], identity=ident[:]),
                     reads=['VcT', 'ident'], writes=['pT'], acc=(c > 0))
            for c, nc_ in chunks:
                S.op('dve', lambda e, c=c, nc_=nc_: e.tensor_copy(out=Vc_sb[0:nc_, c, :], in_=pT[0:nc_, c * 128:(c + 1) * 128]), reads=['pT'], writes=['Vc_sb'], acc=(c > 0))
            off = 255 - 8 * ti
            for kv in range(2):
                for g in range(4):
                    pX = pS if g < 2 else pN
                    pXn = 'pS' if g < 2 else 'pN'
                    S.op('pe', lambda e, kv=kv, g=g, pX=pX: e.matmul(pX[:, (g % 2) * 256:(g % 2) * 256 + N], lhsT=qaTz[kv][:, g, :], rhs=KcT[:, 0:N], start=True, stop=True),
                         reads=['qaTz%d' % kv, 'KcT'], writes=[pXn], acc=(g % 2 == 1))
                for hh, (pX, pXn) in enumerate(((pS, 'pS'), (pN, 'pN'))):
                    S.op('dve', lambda e, kv=kv, hh=hh, pX=pX: e.tensor_tensor(out=sc[:, 2 * hh:2 * hh + 2, 0:N], in0=pX[:, :].rearrange("p (g n) -> p g n", g=2)[:, :, 0:N],
                                                                              in1=E_ext[:, kv * 4 + 2 * hh:kv * 4 + 2 * hh + 2, off:off + N], op=ALU.add),
                         reads=[pXn, 'E_ext'], writes=['sc'], acc=(hh > 0))
                S.op('act', lambda e: e.activation(out=sc[:, :, 0:N], in_=sc[:, :, 0:N], func=AF.Exp), reads=['sc'], writes=['sc'])
                S.op('dve', lambda e: e.tensor_reduce(out=zc[:, 0:4], in_=sc[:, :, 0:N], axis=AX.X, op=ALU.add), reads=['sc'], writes=['zc'])
                S.op('dve', lambda e: e.tensor_scalar_max(out=zc[:, 0:4], in0=zc[:, 0:4], scalar1=tiny), reads=['zc'], writes=['zc'])
                S.op('dve', lambda e: e.reciprocal(out=zc[:, 0:4], in_=zc[:, 0:4]), reads=['zc'], writes=['zc'])
                S.op('dve', lambda e: e.tensor_tensor(out=sc[:, :, 0:N], in0=sc[:, :, 0:N], in1=zc[:, 0:4].unsqueeze(2).to_broadcast([128, 4, N]), op=ALU.mult), reads=['sc', 'zc'], writes=['sc'])
                S.op('dve', lambda e: e.tensor_reduce(out=imp[:, 0:N], in_=sc[:, :, 0:N].rearrange("p g n -> p n g"), axis=AX.X, op=ALU.add), reads=['sc'], writes=['imp'])
                S.op('act', lambda e: e.copy(out=pbf[:, :, 0:N], in_=sc[:, :, 0:N]), reads=['sc'], writes=['pbf'])
                for g in range(4):
                    for c, nc_ in chunks:
                        S.op('pe', lambda e, g=g, c=c, nc_=nc_: e.transpose(out=pT[0:nc_, (g * 2 + c) * 128:(g * 2 + c + 1) * 128], in_=pbf[:, g, c * 128:c * 128 + nc_], identity=ident[:]),
                             reads=['pbf', 'ident'], writes=['pT'], acc=(g > 0 or c > 0))
                for c, nc_ in chunks:
                    S.op('dve', lambda e, c=c, nc_=nc_: e.tensor_copy(out=pcT[0:nc_, :, :].rearrange("p (g c) t -> p g c t", c=2)[:, :, c, :],
                                                                      in_=pT[0:nc_, :].rearrange("p (g c t) -> p g c t", g=4, c=2)[:, :, c, :]),
                         reads=['pT'], writes=['pcT'], acc=(c > 0))
                for g in range(4):
                    for ci_, (c, nc_) in enumerate(chunks):
                        S.op('pe', lambda e, g=g, c=c, nc_=nc_, ci_=ci_, kv=kv: e.matmul(pC[:, g * 64:(g + 1) * 64], lhsT=pcT[0:nc_, g * 2 + c, :], rhs=Vc_sb[0:nc_, c, kv * 64:(kv + 1) * 64],
                                                                                      start=(ci_ == 0), stop=(ci_ == len(chunks) - 1)),
                             reads=['pcT', 'Vc_sb'], writes=['pC'], acc=(g > 0 or ci_ > 0))
                S.op('act', lambda e, kv=kv: e.copy(out=oc[:, 0, kv * 4:(kv + 1) * 4, :], in_=pC[:, 0:256].rearrange("p (g d) -> p g d", g=4)), reads=['pC'], writes=['oc'], acc=True)
                S.skip = DBG_ST < 4
                for c, nc_ in chunks:
                    S.op('pe', lambda e, c=c, nc_=nc_: e.transpose(out=pm[0:nc_, c * 128:(c + 1) * 128], in_=imp[:, c * 128:c * 128 + nc_], identity=ident_f),
                         reads=['imp', 'ident_f'], writes=['pm'], acc=(c > 0))
                for c, nc_ in chunks:
                    S.op('dve', lambda e, c=c, nc_=nc_: e.tensor_copy(out=impT[0:nc_, c, :], in_=pm[0:nc_, c * 128:(c + 1) * 128]), reads=['pm'], writes=['impT'], acc=(c > 0))
                for ci_, (c, nc_) in enumerate(chunks):
                    S.op('pe', lambda e, c=c, nc_=nc_, ci_=ci_: e.matmul(pm[:, 256:320], lhsT=impT[0:nc_, c, :], rhs=cover_f[0:nc_, c, :], start=(ci_ == 0), stop=(ci_ == len(chunks) - 1)),
                         reads=['impT', 'ident_f'], writes=['pm'], acc=(ci_ > 0))
                S.op('dve', lambda e, ti=ti: e.tensor_tensor(out=sco[:], in0=pm[:, 256:320], in1=fv_ext[:, 64 - 2 * ti:128 - 2 * ti], op=ALU.add), reads=['pm', 'ident_f'], writes=['sco'])
                S.op('dve', lambda e: e.tensor_copy(out=sco[:, 0:1], in_=bigc[:]), reads=['bigc', 'sco'], writes=['sco'])
                S.op('dve', lambda e: e.max(out=m8[:, 0:8], in_=sco[:]), reads=['sco'], writes=['m8'])
                S.op('dve', lambda e: e.match_replace(out=wk_[:], in_to_replace=m8[:, 0:8], in_values=sco[:], imm_value=-3.0e30), reads=['sco', 'm8'], writes=['wk_'])
                S.op('dve', lambda e: e.max(out=m8[:, 8:16], in_=wk_[:]), reads=['wk_', 'm8'], writes=['m8'])
                S.op('dve', lambda e, kv=kv: e.tensor_scalar(out=negm[:, kv, :], in0=sco[:], scalar1=m8[:, 15:16], scalar2=-NEG, op0=ALU.is_ge, op1=ALU.mult), reads=['sco', 'm8'], writes=['negm'], acc=(kv > 0))
                S.op('dve', lambda e, kv=kv: e.tensor_scalar_add(out=negm[:, kv, :], in0=negm[:, kv, :], scalar1=NEG), reads=['negm'], writes=['negm'])
            S.skip = DBG_ST < 5
            S.op('pe', lambda e: e.transpose(out=pm[:, 320:448], in_=negm[:].rearrange("p k b -> p (k b)"), identity=ident_f), reads=['negm', 'ident_f'], writes=['pm'])
            S.op('dve', lambda e: e.tensor_copy(out=negmZ[0][0:64, :, :], in_=pm[0:64, 320:448].unsqueeze(1).to_broadcast([64, 4, 128])), reads=['pm'], writes=['negmZ0'])
            S.op('dve', lambda e: e.tensor_copy(out=negmZ[1][64:128, :, :], in_=pm[64:128, 320:448].unsqueeze(1).to_broadcast([64, 4, 128])), reads=['pm'], writes=['negmZ1'])
            S.skip = DBG_ST < 6
            for br, (kts, pO, pOn) in enumerate((((list(range(0, ti + 1))), pS, 'pS'), ((list(range(max(0, ti - 4), ti + 1))), pN, 'pN'))):
                for kv in range(2):
                    pend = None
                    for ii, kt in enumerate(kts):
                        pX = pacc[ii % 2]
                        pXn = 'pacc%d' % (ii % 2)
                        PT_ = PTb[ii % 2]
                        PTn = 'PTb%d' % (ii % 2)
                        extra = []
                        if br == 0:
                            kk = skT[:, kt * 128:(kt + 1) * 128]
                            kkn = 'skT'
                            extra.append((Eblk[:, kt * 128:(kt + 1) * 128], negmZ[kv][:].rearrange("p g t -> p (g t)"), ['Eblk', 'negmZ%d' % kv]))
                            vv_ = svx[:, kt, kv, :]
                            vvn = 'svx'
                        else:
                            kk = wkT[:, kt % 8, :]
                            kkn = 'wkT'
                            if kt == ti - 4:
                                extra.append((ident[:], wm4r[:].rearrange("p g t -> p (g t)"), ['ident', 'wm4r']))
                            vv_ = wvx[:, kt % 8, kv, :]
                            vvn = 'wvx'
                        if kt >= ti - 1:
                            extra.append((ident[:], DT[:, kt - ti + 1, kv * 4:(kv + 1) * 4, :], ['ident', 'DT']))
                        S.op('pe', lambda e, pX=pX, kk=kk, kv=kv, extra=extra: e.matmul(pX[:, :], lhsT=kk, rhs=qaTz[kv][:].rearrange("p g t -> p (g t)"), start=True, stop=(len(extra) == 0)),
                             reads=[kkn, 'qaTz%d' % kv], writes=[pXn])
                        for xi, (l_, r_, rs_) in enumerate(extra):
                            S.op('pe', lambda e, pX=pX, l_=l_, r_=r_, xi=xi, extra=extra: e.matmul(pX[:, :], lhsT=l_, rhs=r_, start=False, stop=(xi == len(extra) - 1)),
                                 reads=rs_, writes=[pXn], acc=True)
                        S.op('act', lambda e, pX=pX, PT_=PT_: e.activation(out=PT_[:], in_=pX[:, :], func=AF.Exp), reads=[pXn], writes=[PTn])
                        if pend is not None:
                            pend()
                        pend = (lambda pO=pO, vv_=vv_, PT_=PT_, ii=ii, kts=kts, vvn=vvn, PTn=PTn, pOn=pOn:
                                S.op('pe', lambda e: e.matmul(pO[0:65, :], lhsT=vv_, rhs=PT_[:], start=(ii == 0), stop=(ii == len(kts) - 1)),
                                     reads=[vvn, PTn], writes=[pOn], acc=(ii > 0)))
                    pend()
                    pend = None
                    S.op('dve', lambda e, pO=pO: e.tensor_copy(out=OTs[:], in_=pO[0:65, :]), reads=[pOn], writes=['OTs'])
                    for g in range(4):
                        S.op('pe', lambda e, g=g: e.transpose(out=pm[:, g * 65:(g + 1) * 65], in_=OTs[:, g * 128:(g + 1) * 128], identity=ident_f[0:65, 0:65]),
                             reads=['OTs', 'ident_f'], writes=['pm'], acc=(g > 0))
                    pmv = pm[:, 0:260].rearrange("p (g d) -> p g d", g=4)
                    S.op('dve', lambda e, pmv=pmv: e.reciprocal(out=zc[:, 4:8], in_=pmv[:, :, 64]), reads=['pm'], writes=['zc'])
                    S.op('dve', lambda e, pmv=pmv, br=br, kv=kv: e.tensor_tensor(out=oc[:, 1 + br, kv * 4:(kv + 1) * 4, :], in0=pmv[:, :, 0:64],
                                                                                 in1=zc[:, 4:8].unsqueeze(2).to_broadcast([128, 4, 64]), op=ALU.mult),
                         reads=['pm', 'zc'], writes=['oc'], acc=True)
            S.skip = DBG_ST < 7
            S.op('act', lambda e: e.activation(out=gsig[:], in_=proj[:, O_GA:O_GA + 24], func=AF.Sigmoid), reads=['proj'], writes=['gsig'])
            ocf = oc[:].rearrange("p b h d -> p (b h) d")
            S.op('dve', lambda e: e.tensor_tensor(out=ocf, in0=ocf, in1=gsig[:].unsqueeze(2).to_broadcast([128, 24, 64]), op=ALU.mult), reads=['oc', 'gsig'], writes=['oc'])
            oav = oall[:].rearrange("p (h d) -> p h d", h=8)
            S.op('pool', lambda e: e.tensor_tensor(out=oav, in0=oc[:, 0, :, :], in1=oc[:, 1, :, :], op=ALU.add), reads=['oc'], writes=['oall'])
            S.op('pool', lambda e: e.tensor_tensor(out=oav, in0=oav, in1=oc[:, 2, :, :], op=ALU.add), reads=['oc', 'oall'], writes=['oall'])
            S.op('act', lambda e: e.activation(out=sig[:, 512:1024], in_=proj[:, O_ZA:O_ZA + 512], func=AF.Sigmoid), reads=['proj', 'mix'], writes=['sig'])
            S.op('pool', lambda e: e.tensor_tensor(out=sig[:, 512:1024], in0=sig[:, 512:1024], in1=proj[:, O_ZA:O_ZA + 512], op=ALU.mult), reads=['sig', 'proj'], writes=['sig'])
            S.op('dve', lambda e: e.tensor_tensor(out=mix[:, 512:1024], in0=oall[:], in1=sig[:, 512:1024], op=ALU.mult), reads=['oall', 'sig'], writes=['mix'], acc=True)
            S.skip = DBG_ST < 8
            for k in range(8):
                S.op('pe', lambda e, k=k: e.transpose(out=pT[:, k * 128:(k + 1) * 128], in_=mix[:, k * 128:(k + 1) * 128], identity=ident[:]),
                     reads=['mix', 'ident'], writes=['pT'], acc=(k > 0))
            S.op('act', lambda e: e.activation(out=mixT[:], in_=pT[:].rearrange("p (k t) -> p k t", k=8), func=AF.Identity), reads=['pT'], writes=['mixT'])
            for c in range(2):
                for k in range(8):
                    S.op('pe', lambda e, c=c, k=k: e.matmul(pacc[c][:, :], lhsT=mixT[:, k, :], rhs=w_out_bf[:, k, c * 512:(c + 1) * 512], start=(k == 0), stop=(k == 7)),
                         reads=['mixT', 'w_out_bf'], writes=['pacc%d' % c], acc=(k > 0))
                S.op('dve', lambda e, c=c: e.tensor_tensor(out=sig[:, c * 512:(c + 1) * 512], in0=pacc[c][:, :], in1=bout_bc[:, c * 512:(c + 1) * 512], op=ALU.add),
                     reads=['pacc%d' % c, 'bout_bc', 'mix'], writes=['sig'], acc=(c > 0))
            S.op('dve', lambda e, xb=xb: e.scalar_tensor_tensor(out=sig[:], in0=xb[:], scalar=ALPHA, in1=sig[:], op0=ALU.mult, op1=ALU.add), reads=[xnm, 'sig'], writes=['sig'])
            layer_norm_stats(sig, 'sig', 128, mvp, st6p, 'p')
            S.op('dve', lambda e: e.tensor_scalar(out=sig[:], in0=sig[:], scalar1=mvp[:, 0:1], scalar2=mvp[:, 2:3], op0=ALU.subtract, op1=ALU.mult), reads=['sig', 'mvp'], writes=['sig'])
            S.op('pool', lambda e: e.tensor_tensor(out=sig[:], in0=sig[:], in1=lng_bc[:], op=ALU.mult), reads=['sig', 'lng_bc'], writes=['sig'])
            S.op('dve', lambda e: e.tensor_tensor(out=sig[:], in0=sig[:], in1=lnb_bc[:], op=ALU.add), reads=['sig', 'lnb_bc'], writes=['sig'])
            S.dma('pool', lambda e, ti=ti: e.dma_start(out=o_y_p[ti * 128:(ti + 1) * 128, :], in_=sig[:]), 'sig', reads=['sig'])
            S.skip = False
            for br in range(3):
                src = proj[:, O_CK + br * 256:O_CK + (br + 1) * 256].rearrange("p (c k d) -> p k c d", c=2, k=2)
                dst = kvp[:, br * 256:(br + 1) * 256].rearrange("p (k c d) -> p k c d", k=2, c=2)
                S.op('pool', lambda e, src=src, dst=dst: e.tensor_copy(out=dst, in_=src), reads=['proj'], writes=['kvp'], acc=(br > 0))
            S.dma('sp', lambda e, ti=ti: e.dma_start(out=o_cmp_p[ti * 128:(ti + 1) * 128, :], in_=kvp[:, 0:256]), 'kvp', reads=['kvp'])
            S.dma('sp', lambda e, ti=ti: e.dma_start(out=o_slc_p[ti * 128:(ti + 1) * 128, :], in_=kvp[:, 256:512]), 'kvp', reads=['kvp'])
            if ti >= NT - 4:
                r0 = (ti - (NT - 4)) * 128
                S.dma('sp', lambda e, r0=r0: e.dma_start(out=o_win_p[r0:r0 + 128, :], in_=kvp[:, 512:768]), 'kvp', reads=['kvp'])
        S.op('pe', lambda e: e.transpose(out=pm[0:4, 0:128], in_=runmax[:], identity=ident_f), reads=['runmax', 'ident_f'], writes=['pm'])
        mx4 = sb("mx4", [4, 1], st=ph)
        tmp4 = sb("tmp4", [4, 128], st=ph)
        S.op('dve', lambda e: e.reduce_max(out=mx4[:], in_=pm[0:4, 0:128], axis=AX.X), reads=['pm'], writes=['mx4'])
        S.op('dve', lambda e: e.tensor_scalar(out=tmp4[:], in0=onesf[0:4, :], scalar1=mx4[:, 0:1], scalar2=None, op0=ALU.mult), reads=['mx4', 'onesf'], writes=['tmp4'])
        S.op('pe', lambda e: e.matmul(pm[:, 8:12], lhsT=tmp4[:], rhs=ident_f[0:4, 0:4], start=True, stop=True), reads=['tmp4', 'ident_f'], writes=['pm'])
        mfin = sb("mfin", [128, 8], st=ph)
        S.op('dve', lambda e: e.tensor_tensor(out=mfin[:, 0:4], in0=pm[:, 8:12], in1=NG[:], op=ALU.subtract), reads=['pm', 'NG'], writes=['mfin'])
        S.op('act', lambda e: e.activation(out=mfin[:, 4:8], in_=mfin[:, 0:4], func=AF.Exp, scale=-1.0), reads=['mfin'], writes=['mfin'])
        S.op('dve', lambda e: e.tensor_tensor(out=Cst[:], in0=Cst[:], in1=mfin[:, 4:8].unsqueeze(2).to_broadcast([128, 4, 129]), op=ALU.mult), reads=['Cst', 'mfin'], writes=['Cst'])
        S.dma('sp', lambda e: e.dma_start(out=o_C_p.rearrange("h d e -> d h e"), in_=Cst[:, :, 0:128]), 'Cst', reads=['Cst'])
        nout = sb("nout", [128, 4], st=ph)
        S.op('dve', lambda e: e.tensor_copy(out=nout[:], in_=Cst[:, :, 128]), reads=['Cst'], writes=['nout'])
        S.dma('sp', lambda e: e.dma_start(out=o_n_p.rearrange("h d -> d h"), in_=nout[:], allow_slow_non_contiguous=True), 'nout', reads=['nout'])
        S.dma('sp', lambda e: e.dma_start(out=o_m_p, in_=mfin[0:1, 0:4]), 'mfin', reads=['mfin'])
        S.barrier()

    S.finish()
    S.emit()
    es.close()
    return nc


_NC_CACHE = {}


def kernel(x_prompt, x_sample, cache_cmp_kv, cache_slc_kv, cache_win_kv,
           state_mlstm_C, state_mlstm_n, state_mlstm_m, page_table,
           c_prompt, c_sample, rel_bias, w_ada, b_ada, w_in, b_in, m_norm_g,
           cmp_pe, cmp_w1, cmp_b1, cmp_w2, w_out, b_out, ln_g, ln_b):
    f = np.float32
    if 'nc' not in _NC_CACHE:
        _NC_CACHE['nc'] = build_nc()
    nc = _NC_CACHE['nc']
    x_prompt = np.asarray(x_prompt, f)
    x_sample = np.asarray(x_sample, f).reshape(128, D)
    c_prompt = np.asarray(c_prompt, f)
    c_sample = np.asarray(c_sample, f)
    win_c = np.asarray(cache_win_kv, f).reshape(128, 512, 256)
    consts = np.zeros((128, NCONST), f)
    consts[:, 0:128] = np.eye(128, dtype=f)
    consts[:, 128:256] = np.triu(np.ones((128, 128), f))
    consts[:, 256:512] = np.eye(16, dtype=f).reshape(1, 256)
    n_i = np.arange(256)[:, None]
    b_i = np.arange(64)[None, :]
    cover = ((16 * n_i <= 64 * b_i + 63) & (16 * n_i + 31 >= 64 * b_i) & (n_i < 255)).astype(f)
    consts[:, 512:640] = cover.reshape(2, 128, 64).transpose(1, 0, 2).reshape(128, 128)
    p_i = np.arange(128)[:, None]
    rel = np.arange(128)[None, :] - 64
    cr = (p_i >= 64).astype(np.int64)
    fv = np.where((rel == cr) | (rel == cr - 1), BIG, np.where(rel > cr, -BIG, 0.0))
    consts[:, 640:768] = fv.astype(f)

    def bucket(dist):
        n = np.maximum(dist, 0)
        nf = np.maximum(n, 1).astype(np.float32)
        large = 16 + (np.log(nf / np.float32(16)) / np.float32(np.log(8.0)) * np.float32(16)).astype(np.int32)
        large = np.minimum(large, 31)
        return np.where(n < 16, n, large)

    cc_q = np.asarray(cache_cmp_kv, f).reshape(NPHYS * 4, 8192)
    cs_q = np.asarray(cache_slc_kv, f).reshape(NPHYS * 4, 8192)
    consts_s = np.zeros((128, NCS), f)
    pp_ = np.arange(128)
    consts_s[:, 0:32] = (np.arange(32)[None, :] == (pp_[:, None] // 4)).astype(f)
    consts_s[:, 32] = (pp_ % 4).astype(f)
    n_s = np.arange(512)[:, None]
    b_s = np.arange(129)[None, :]
    cov_s = ((16 * n_s <= 64 * b_s + 63) & (16 * n_s + 31 >= 64 * b_s) & (n_s < 511)).astype(f)
    consts_s[:, 33:549] = cov_s.reshape(4, 128, 129).transpose(1, 0, 2).reshape(128, 516)
    j_ = np.arange(128)[:, None]
    r32 = np.arange(32)[None, :]
    dn_ = 4096 - 32 * j_ - r32
    bn_ = bucket(dn_)
    ohn_ = np.stack([(bn_ == b).astype(f) for b in range(32)], axis=1)
    consts_s[:, 549:1573] = ohn_.reshape(128, 1024)
    r4 = np.arange(4)[None, :]
    dw_ = 512 - 4 * j_ - r4
    bw_ = bucket(dw_)
    ohw_ = np.stack([(bw_ == b).astype(f) for b in range(32)], axis=1)
    consts_s[:, 1573:1701] = ohw_.reshape(128, 128)
    dcs = 8192 - (16 * np.arange(512) + 31)
    bcs_ = bucket(dcs)
    ohcs = np.stack([(bcs_ == b).astype(f) for b in range(32)], axis=0)
    eblk = ((np.arange(T)[None, :] // 64) == (np.arange(128)[:, None] % 64)).astype(f)
    s_i = np.arange(128)[:, None]
    t_i = np.arange(128)[None, :]
    oh_dt = np.zeros((2, 128, 32, 128), f)
    for v in range(2):
        dist = t_i - s_i + (128 if v == 0 else 0)
        bk = bucket(dist)
        for b in range(32):
            oh_dt[v, :, b, :] = ((bk == b) & (dist >= 0)).astype(f)
    am_dt = np.zeros((128, 3, 128), f)
    am_dt[:, 0, :] = np.where(t_i - s_i >= 0, 0.0, NEG)
    am_dt[:, 2, :] = np.where(s_i >= t_i, 0.0, NEG)
    m_i = np.arange(16)[None, :] - 9
    distc = p_i - 16 * m_i - 31
    bkc = bucket(distc)
    oh_c = np.zeros((128, 32, 16), f)
    for b in range(32):
        oh_c[:, b, :] = ((bkc == b) & (distc >= 0)).astype(f)
    am_c = np.where(distc >= 0, 0.0, NEG).astype(f)
    in_maps = []
    for c in range(NCORES):
        b = c % 4
        in_maps.append({
            "x_p": np.ascontiguousarray(x_prompt[b]),
            "x_s": np.ascontiguousarray(x_sample[c * SB:(c + 1) * SB]),
            "c_p": np.ascontiguousarray(c_prompt[b:b + 1]),
            "c_s": np.ascontiguousarray(c_sample[c * SB:(c + 1) * SB]),
            "w_ada": np.asarray(w_ada, f).reshape(D, 3 * D),
            "b_ada": np.asarray(b_ada, f).reshape(1, 3 * D),
            "w_in": np.asarray(w_in, f).reshape(D, DIN),
            "b_in": np.asarray(b_in, f).reshape(1, DIN),
            "consts_in": consts,
            "m_norm_g": np.asarray(m_norm_g, f).reshape(1, 512),
            "cache_cmp": cc_q,
            "cache_slc": cs_q,
            "page_tab": np.ascontiguousarray(np.asarray(page_table, np.int32)[c * SB:(c + 1) * SB].reshape(1, SB * 64)),
            "consts_s": consts_s,
            "ohcs_in": ohcs,
            "rel_bias": np.asarray(rel_bias, f).reshape(1, 256),
            "eblk": eblk,
            "oh_dt": oh_dt.reshape(2, 128, 32 * 128),
            "am_dt": am_dt.reshape(128, 3 * 128),
            "oh_c": oh_c.reshape(128, 32 * 16),
            "am_c": am_c,
            "cmp_w1": np.asarray(cmp_w1, f).reshape(2, 32, 64, 64),
            "cmp_pe": np.asarray(cmp_pe, f).reshape(2, 32, 64),
            "cmp_b1": np.asarray(cmp_b1, f).reshape(2, 64),
            "cmp_w2": np.asarray(cmp_w2, f).reshape(2, 64, 64),
            "w_out": np.asarray(w_out, f).reshape(D, D),
            "b_out": np.asarray(b_out, f).reshape(1, D),
            "ln_g": np.asarray(ln_g, f).reshape(1, D),
            "ln_b": np.asarray(ln_b, f).reshape(1, D),
            "st_C": np.ascontiguousarray(np.asarray(state_mlstm_C, f)[0, c * SB:(c + 1) * SB]),
            "st_n": np.ascontiguousarray(np.asarray(state_mlstm_n, f)[0, c * SB:(c + 1) * SB].reshape(SB, 512)),
            "st_m": np.ascontiguousarray(np.asarray(state_mlstm_m, f)[0, c * SB:(c + 1) * SB]),
            "win_c": np.ascontiguousarray(win_c[c * SB:(c + 1) * SB]),
        })
    res = run_bass_kernel_spmd(nc, in_maps, core_ids=list(range(NCORES)))
    R = res.results

    def cat_p(name, shape):
        return np.stack([np.asarray(R[b][name], f).reshape(shape) for b in range(4)], axis=0)

    def cat_s(name, shape):
        return np.concatenate([np.asarray(R[c][name], f).reshape((SB,) + shape) for c in range(NCORES)], axis=0)

    y_p = cat_p("o_y_p", (T, D))
    y_s = cat_s("o_y_s", (1, D))
    cmp_p = cat_p("o_cmp_p", (T, 2, 2, 64))[None]
    slc_p = cat_p("o_slc_p", (T, 2, 2, 64))[None]
    win_p = cat_p("o_win_p", (512, 2, 2, 64))[None]
    cmp_s = cat_s("o_cmp_s", (1, 2, 2, 64))[None]
    slc_s = cat_s("o_slc_s", (1, 2, 2, 64))[None]
    win_s = cat_s("o_win_s", (512, 2, 2, 64))[None]
    C_p = cat_p("o_C_p", (4, 128, 128))[None]
    C_s = cat_s("o_C_s", (4, 128, 128))[None]
    n_p = cat_p("o_n_p", (4, 128))[None]
    n_s = cat_s("o_n_s", (4, 128))[None]
    m_p = cat_p("o_m_p", (4,))[None]
    m_s = cat_s("o_m_s", (4,))[None]
    return (y_p, y_s, cmp_p, cmp_s, slc_p, slc_s, win_p, win_s, C_p, C_s, n_p, n_s, m_p, m_s)
```
